# Optimizing a Trainium2 kernel written in Bass

```python
import jax, jax.numpy as jnp
from jax import lax
import numpy as np

D_MODEL = 1024
BATCH = 8
SEQ = 2048
DEPTH = 1
DEC_BATCH = 16
DEC_SEQ = 16
PAST_LEN = 2048

CHUNK = 64
RET_HEADS = 4
RET_DK = 128
RET_DV = 128
RET_THETA = 10000.0
RET_W = RET_HEADS * RET_DV
SWA_HEADS = 8
SWA_KV_HEADS = 2
SWA_HD = 64
SWA_REP = SWA_HEADS // SWA_KV_HEADS
SWA_WINDOW = 128
SWA_NB = SWA_WINDOW // CHUNK
SWA_ROT_DIM = SWA_HD // 4
SWA_THETA = 500000.0
SWA_W = SWA_HEADS * SWA_HD
MIX_W = RET_W + SWA_W
D_FF = 4 * D_MODEL
PROJ_SPLITS = (RET_HEADS * RET_DK, RET_HEADS * RET_DK, RET_W, RET_W, SWA_W, SWA_KV_HEADS * SWA_HD, SWA_KV_HEADS * SWA_HD)
PROJ_OFFSETS = tuple(int(o) for o in np.cumsum(PROJ_SPLITS)[:-1])
PROJ_W = int(sum(PROJ_SPLITS))
ALPHA = (2.0 * DEPTH) ** 0.25
BETA = (8.0 * DEPTH) ** -0.25
LN_EPS = 1e-5
GN_EPS = 1e-5
NEG_INF = -1e30

kernel_name = "hybrid_retention_swa_streaming_step"


def rope(x, pos, rot_dim, theta):
    half = rot_dim // 2
    inv = theta ** (-jnp.arange(half, dtype=jnp.float32) / half)
    ang = pos.astype(jnp.float32)[:, None] * inv[None, :]
    cos = jnp.cos(ang)[:, None, :]
    sin = jnp.sin(ang)[:, None, :]
    xr = x[..., :rot_dim].astype(jnp.float32)
    x1, x2 = xr[..., :half], xr[..., half:]
    rot = jnp.concatenate([x1 * cos - x2 * sin, x2 * cos + x1 * sin], axis=-1).astype(x.dtype)
    return jnp.concatenate([rot, x[..., rot_dim:]], axis=-1)


def layer_norm(x, w, b):
    xf = x.astype(jnp.float32)
    mu = xf.mean(-1, keepdims=True)
    var = jnp.square(xf - mu).mean(-1, keepdims=True)
    return ((xf - mu) * lax.rsqrt(var + LN_EPS) * w + b).astype(x.dtype)


def in_proj(x, pos, w_in):
    b, t, _ = x.shape
    p = jnp.einsum('btd,de->bte', x, w_in)
    rq, rk, rv, rg, sq, sk, sv = jnp.split(p, PROJ_OFFSETS, axis=-1)
    rq = rope(rq.reshape(b, t, RET_HEADS, RET_DK), pos, RET_DK, RET_THETA)
    rk = rope(rk.reshape(b, t, RET_HEADS, RET_DK) * (RET_DK ** -0.5), pos, RET_DK, RET_THETA)
    rv = rv.reshape(b, t, RET_HEADS, RET_DV)
    sq = rope(sq.reshape(b, t, SWA_HEADS, SWA_HD), pos, SWA_ROT_DIM, SWA_THETA)
    sk = rope(sk.reshape(b, t, SWA_KV_HEADS, SWA_HD), pos, SWA_ROT_DIM, SWA_THETA)
    sv = sv.reshape(b, t, SWA_KV_HEADS, SWA_HD)
    return rq, rk, rv, rg, sq, sk, sv


def ret_log_gamma():
    return jnp.log1p(-(2.0 ** (-5.0 - jnp.arange(RET_HEADS, dtype=jnp.float32))))


def retention_chunk(s, q, k, v):
    t = q.shape[1]
    lg = ret_log_gamma()
    idx = jnp.arange(t, dtype=jnp.float32)
    diff = idx[:, None] - idx[None, :]
    dmask = jnp.where(diff >= 0, jnp.exp(lg[:, None, None] * jnp.maximum(diff, 0.0)), 0.0)
    scores = jnp.einsum('bnhd,bmhd->bhnm', q, k) * dmask
    o = jnp.einsum('bhnm,bmhe->bnhe', scores, v)
    q_decay = jnp.exp(lg[None, :] * (idx[:, None] + 1.0))
    o = o + jnp.einsum('bnhd,bhde->bnhe', q, s) * q_decay[None, :, :, None]
    k_decay = jnp.exp(lg[None, :] * (t - 1.0 - idx[:, None]))
    s_new = jnp.exp(lg * t)[None, :, None, None] * s + jnp.einsum('bmhd,bmhe->bhde', k * k_decay[None, :, :, None], v)
    return o, s_new


def retention_prompt(q, k, v):
    b, L = q.shape[:2]
    nc = L // CHUNK

    def to_chunks(a):
        return a.astype(jnp.float32).reshape(b, nc, CHUNK, RET_HEADS, -1).transpose(1, 0, 2, 3, 4)

    def step(s, qkv):
        o, s_new = retention_chunk(s, *qkv)
        return s_new, o

    s0 = jnp.zeros((b, RET_HEADS, RET_DK, RET_DV), jnp.float32)
    s_fin, o = lax.scan(step, s0, (to_chunks(q), to_chunks(k), to_chunks(v)))
    o = o.transpose(1, 0, 2, 3, 4).reshape(b, L, RET_HEADS, RET_DV)
    return o, s_fin


def retention_out(o, g, gn_w):
    b, t = o.shape[:2]
    mu = o.mean(-1, keepdims=True)
    var = jnp.square(o - mu).mean(-1, keepdims=True)
    on = ((o - mu) * lax.rsqrt(var + GN_EPS)).reshape(b, t, RET_W) * gn_w
    return (jax.nn.silu(g.astype(jnp.float32)) * on).astype(g.dtype)


def sink_softmax(s, sinks):
    snk = jnp.broadcast_to(sinks.astype(jnp.float32).reshape(SWA_KV_HEADS, SWA_REP, 1, 1), s.shape[:-1] + (1,))
    p = jax.nn.softmax(jnp.concatenate([s, snk], axis=-1), axis=-1)
    return p[..., :-1]


def swa_prompt(q, k, v, sinks):
    b, L = q.shape[:2]
    nc = L // CHUNK
    qc = q.reshape(b, nc, CHUNK, SWA_KV_HEADS, SWA_REP, SWA_HD)

    def band(a):
        ap = jnp.pad(a, ((0, 0), (SWA_WINDOW, 0), (0, 0), (0, 0))).reshape(b, nc + SWA_NB, CHUNK, SWA_KV_HEADS, SWA_HD)
        return jnp.concatenate([ap[:, j:j + nc] for j in range(SWA_NB + 1)], axis=2)

    kb, vb = band(k), band(v)
    kb_len = (SWA_NB + 1) * CHUNK
    kpos = jnp.arange(nc)[:, None] * CHUNK - SWA_WINDOW + jnp.arange(kb_len)[None, :]
    valid = kpos >= 0
    s = jnp.einsum('bcqgrd,bckgd->bcgrqk', qc, kb).astype(jnp.float32) * (SWA_HD ** -0.5)
    s = jnp.where(valid[None, :, None, None, None, :], s, NEG_INF)
    p = sink_softmax(s, sinks)
    o = jnp.einsum('bcgrqk,bckgd->bcqgrd', p.astype(v.dtype), vb)
    return o.reshape(b, L, SWA_W)


def swa_sample(q, k_new, v_new, cache_k, cache_v, sinks):
    b, t = q.shape[:2]
    kk = jnp.concatenate([cache_k.astype(k_new.dtype), k_new], axis=1)
    vv = jnp.concatenate([cache_v.astype(v_new.dtype), v_new], axis=1)
    qg = q.reshape(b, t, SWA_KV_HEADS, SWA_REP, SWA_HD)
    s = jnp.einsum('btgrd,bkgd->bgrtk', qg, kk).astype(jnp.float32) * (SWA_HD ** -0.5)
    p = sink_softmax(s, sinks)
    o = jnp.einsum('bgrtk,bkgd->btgrd', p.astype(vv.dtype), vv)
    return o.reshape(b, t, SWA_W)


def post_block(x, mix, w_out, ln1_w, ln1_b, w_up, w_down, ln2_w, ln2_b):
    x = layer_norm(ALPHA * x + jnp.einsum('btm,md->btd', mix, w_out), ln1_w, ln1_b)
    h = jnp.square(jax.nn.relu(jnp.einsum('btd,df->btf', x, w_up)))
    return layer_norm(ALPHA * x + jnp.einsum('btf,fd->btd', h, w_down), ln2_w, ln2_b)


def setup_inputs(seed: int = 0) -> dict:
    key = jax.random.key(seed)
    ks = jax.random.split(key, 16)
    win = min(SWA_WINDOW, PAST_LEN)
    f32 = jnp.float32
    return {
        "x_prompt": jax.random.normal(ks[0], (BATCH, SEQ, D_MODEL), f32),
        "x_sample": jax.random.normal(ks[1], (DEC_BATCH, DEC_SEQ, D_MODEL), f32),
        "cache_swa_k": jax.random.normal(ks[2], (DEPTH, DEC_BATCH, win, SWA_KV_HEADS, SWA_HD), f32),
        "cache_swa_v": jax.random.normal(ks[3], (DEPTH, DEC_BATCH, win, SWA_KV_HEADS, SWA_HD), f32),
        "state_ret": 0.5 * jax.random.normal(ks[4], (DEPTH, DEC_BATCH, RET_HEADS, RET_DK, RET_DV), f32),
        "w_in": jax.random.normal(ks[5], (DEPTH, D_MODEL, PROJ_W), f32) * D_MODEL ** -0.5,
        "ret_gn_w": 1.0 + 0.02 * jax.random.normal(ks[6], (DEPTH, RET_W), f32),
        "swa_sinks": 0.5 * jax.random.normal(ks[7], (DEPTH, SWA_HEADS), f32),
        "w_out": jax.random.normal(ks[8], (DEPTH, MIX_W, D_MODEL), f32) * (MIX_W ** -0.5) * BETA,
        "ln1_w": 1.0 + 0.02 * jax.random.normal(ks[9], (DEPTH, D_MODEL), f32),
        "ln1_b": 0.02 * jax.random.normal(ks[10], (DEPTH, D_MODEL), f32),
        "w_up": jax.random.normal(ks[11], (DEPTH, D_MODEL, D_FF), f32) * D_MODEL ** -0.5,
        "w_down": jax.random.normal(ks[12], (DEPTH, D_FF, D_MODEL), f32) * (D_FF ** -0.5) * BETA,
        "ln2_w": 1.0 + 0.02 * jax.random.normal(ks[13], (DEPTH, D_MODEL), f32),
        "ln2_b": 0.02 * jax.random.normal(ks[14], (DEPTH, D_MODEL), f32),
    }


def reference(x_prompt, x_sample, cache_swa_k, cache_swa_v, state_ret, w_in, ret_gn_w, swa_sinks,
              w_out, ln1_w, ln1_b, w_up, w_down, ln2_w, ln2_b):
    pos_p = jnp.arange(x_prompt.shape[1])
    pos_s = PAST_LEN + jnp.arange(x_sample.shape[1])
    xp, xs = x_prompt, x_sample
    k_p, v_p, r_p, k_s, v_s, r_s = [], [], [], [], [], []
    for l in range(DEPTH):
        rq, rk, rv, rg, sq, sk, sv = in_proj(xp, pos_p, w_in[l])
        o_ret, s_fin = retention_prompt(rq, rk, rv)
        mix = jnp.concatenate([retention_out(o_ret, rg, ret_gn_w[l]), swa_prompt(sq, sk, sv, swa_sinks[l])], axis=-1)
        xp_new = post_block(xp, mix, w_out[l], ln1_w[l], ln1_b[l], w_up[l], w_down[l], ln2_w[l], ln2_b[l])
        k_p.append(sk[:, -SWA_WINDOW:])
        v_p.append(sv[:, -SWA_WINDOW:])
        r_p.append(s_fin.astype(xp.dtype))
        xp = xp_new
        rq, rk, rv, rg, sq, sk, sv = in_proj(xs, pos_s, w_in[l])
        o_ret, s_new = retention_chunk(state_ret[l].astype(jnp.float32), rq.astype(jnp.float32),
                                       rk.astype(jnp.float32), rv.astype(jnp.float32))
        mix = jnp.concatenate([retention_out(o_ret, rg, ret_gn_w[l]),
                               swa_sample(sq, sk, sv, cache_swa_k[l], cache_swa_v[l], swa_sinks[l])], axis=-1)
        xs_new = post_block(xs, mix, w_out[l], ln1_w[l], ln1_b[l], w_up[l], w_down[l], ln2_w[l], ln2_b[l])
        k_s.append(sk)
        v_s.append(sv)
        r_s.append(s_new.astype(state_ret.dtype))
        xs = xs_new
    return (xp, xs, jnp.stack(k_p), jnp.stack(v_p), jnp.stack(r_p), jnp.stack(k_s), jnp.stack(v_s), jnp.stack(r_s))
```

```python
import contextlib
import numpy as np
import concourse.bass as bass
import concourse.mybir as mybir
from concourse.bass_utils import run_bass_kernel_spmd

F32 = mybir.dt.float32
BF16 = mybir.dt.bfloat16
AF = mybir.ActivationFunctionType
ALU = mybir.AluOpType

ENGS = ("pe", "act", "dve", "pool", "sp")

D = 1024
SEQ = 2048
NTP = 16
NT = 17
PROJ_W = 2816
D_FF = 4096
ALPHA = 2.0 ** 0.25
LN_EPS = 1e-5
GN_EPS = 1e-5
PAST = 2048


class Op:
    __slots__ = ("eng", "meth", "args", "kw", "deps", "is_dma", "needs_inc", "sem", "val")

    def __init__(self, eng, meth, args, kw, is_dma):
        self.eng = eng
        self.meth = meth
        self.args = args
        self.kw = kw
        self.deps = []
        self.is_dma = is_dma
        self.needs_inc = is_dma
        self.sem = None
        self.val = None


class Res:
    __slots__ = ("name", "last_w", "readers")

    def __init__(self, name):
        self.name = name
        self.last_w = None
        self.readers = []


class Prog:
    def __init__(self, n_dma_sems=64):
        self.ops = []
        self.n_dma_sems = n_dma_sems
        self.res = {}
        self.barrier_op = None

    def R(self, name):
        r = self.res.get(name)
        if r is None:
            r = Res(name)
            r.last_w = self.barrier_op
            self.res[name] = r
        return r

    def op(self, eng, meth, args, kw, reads=(), writes=(), dma=False):
        o = Op(eng, meth, args, kw, dma)
        reads = [self.R(r) if isinstance(r, str) else r for r in reads]
        writes = [self.R(w) if isinstance(w, str) else w for w in writes]
        deps = []
        for r in reads:
            if r.last_w is not None:
                deps.append(r.last_w)
            if r.name.startswith("pb"):
                deps.extend(x for x in r.readers if x.eng != eng)
        for w in writes:
            if w.last_w is not None:
                deps.append(w.last_w)
            deps.extend(w.readers)
        for r in reads:
            r.readers.append(o)
        for w in writes:
            w.last_w = o
            w.readers = []
        seen = set()
        for d in deps:
            if d is o or id(d) in seen:
                continue
            if (not d.is_dma) and (not dma) and d.eng == "pe" and eng == "pe":
                continue
            seen.add(id(d))
            o.deps.append(d)
            d.needs_inc = True
        self.ops.append(o)
        return o

    def barrier(self, skip=()):
        o = self.op("sp", "nop", (), {"nofuse": True}, reads=(),
                    writes=[r for r in self.res.values() if not r.name.startswith(tuple(skip))])
        self.barrier_op = o
        o.needs_inc = True
        return o

    def emit(self, nc, finals):
        import os as _os
        _tr = int(_os.environ.get("KTRUNC", "0"))
        if _tr:
            keep = set(id(o) for o in self.ops[:_tr])
            self.ops = self.ops[:_tr]
            finals = [f for f in finals if id(f) in keep]
            print("TRUNCATED to", _tr, "ops; last:", self.ops[-1].eng, self.ops[-1].meth)
        ops = self.ops
        st = contextlib.ExitStack()
        eng_sem = {e: st.enter_context(nc.semaphore("s_" + e)) for e in ENGS}
        dma_sems = [st.enter_context(nc.semaphore("d%d" % i)) for i in range(self.n_dma_sems)]
        cnt = {e: 0 for e in ENGS}
        dma_use = [0] * self.n_dma_sems
        dma_prev = [None] * self.n_dma_sems
        half = self.n_dma_sems // 2
        kq = {"pool": 0, "hw": 0}
        for o in ops:
            if o.is_dma:
                if o.eng == "pool":
                    s = kq["pool"] % half
                    kq["pool"] += 1
                else:
                    s = half + kq["hw"] % (self.n_dma_sems - half)
                    kq["hw"] += 1
                dma_use[s] += 1
                o.sem = ("d", s)
                o.val = 16 * dma_use[s]
                if dma_prev[s] is not None:
                    o.deps.append(dma_prev[s])
                dma_prev[s] = o
            elif o.needs_inc:
                cnt[o.eng] += 1
                o.sem = ("e", o.eng)
                o.val = cnt[o.eng]

        def semh(key):
            return eng_sem[key[1]] if key[0] == "e" else dma_sems[key[1]]

        vc_of = {}
        eng_known = {e: {} for e in ENGS}
        plans = {e: [] for e in ENGS}
        for o in ops:
            kn = eng_known[o.eng]
            wm = {}
            for d in o.deps:
                key, val = d.sem, d.val
                if kn.get(key, 0) >= val:
                    continue
                if wm.get(key, 0) < val:
                    wm[key] = val
                for k2, v2 in vc_of[id(d)].items():
                    if kn.get(k2, 0) < v2:
                        kn[k2] = v2
            plans[o.eng].append((list(wm.items()), o))
            if o.sem is not None:
                v = dict(kn)
                if v.get(o.sem, 0) < o.val:
                    v[o.sem] = o.val
                vc_of[id(o)] = v
        self.stats = {e: len(plans[e]) for e in ENGS}
        self.stats["waits"] = sum(len(w) for e in ENGS for w, _ in plans[e])

        with st, nc.Block() as block:
            def mk(ename):
                def body(eng):
                    for waits, o in plans[ename]:
                        for key, val in waits:
                            eng.wait_ge(semh(key), val)
                        ins = getattr(eng, o.meth)(*o.args, **o.kw)
                        if o.sem is not None:
                            ins.then_inc(semh(o.sem), 16 if o.is_dma else 1)
                    if ename == "sp":
                        for f in finals:
                            eng.wait_ge(semh(f.sem), f.val)
                return body
            block.tensor(mk("pe"))
            block.scalar(mk("act"))
            block.vector(mk("dve"))
            block.gpsimd(mk("pool"))
            block.sync(mk("sp"))


CT = {}
_off = 0
for _n, _w in [("cm", 128), ("cm_s", 64), ("nb_s", 2), ("nhalf", 4)]:
    CT[_n] = (_off, _w)
    _off += _w
NCT = _off
RTW = 1040


def _lg():
    return np.log1p(-(2.0 ** (-5.0 - np.arange(4, dtype=np.float64))))


def make_tables():
    lg = _lg()
    tab = np.zeros((128, NCT), np.float64)

    def put(name, arr):
        o, w = CT[name]
        arr = np.asarray(arr, np.float64)
        tab[:arr.shape[0], o:o + w] = arr.reshape(arr.shape[0], w)

    p = np.arange(128)
    inv_r = (10000.0 ** (-np.arange(64, dtype=np.float32) / 64)).astype(np.float32)
    inv_s = (500000.0 ** (-np.arange(8, dtype=np.float32) / 8)).astype(np.float32)
    rt = np.zeros((NT, 128, RTW), np.float64)
    pos = (np.arange(NTP)[:, None] * 128 + p[None, :]).astype(np.float32)
    ang_r = (pos[:, :, None] * inv_r[None, None, :]).astype(np.float32).astype(np.float64)
    ang_s = (pos[:, :, None] * inv_s[None, None, :]).astype(np.float32).astype(np.float64)
    qdec = np.exp(lg[None, :] * (p[:, None] - 127.0))
    kdec = (128.0 ** -0.5) * np.exp(lg[None, :] * (127.0 - p[:, None]))
    cr, sr = np.cos(ang_r), np.sin(ang_r)
    rt[:NTP, :, 0:256] = (cr[:, :, None, :] * qdec[None, :, :, None]).reshape(NTP, 128, 256)
    rt[:NTP, :, 256:512] = (sr[:, :, None, :] * qdec[None, :, :, None]).reshape(NTP, 128, 256)
    rt[:NTP, :, 512:768] = (cr[:, :, None, :] * kdec[None, :, :, None]).reshape(NTP, 128, 256)
    rt[:NTP, :, 768:1024] = (sr[:, :, None, :] * kdec[None, :, :, None]).reshape(NTP, 128, 256)
    rt[:NTP, :, 1024:1032] = np.cos(ang_s)
    rt[:NTP, :, 1032:1040] = np.sin(ang_s)
    rows = np.arange(64)
    i_loc = rows % 32
    valid = i_loc < 16
    b_of = rows // 32
    pos_s = (PAST + np.minimum(i_loc, 15)).astype(np.float32)
    ang_rs = (pos_s[:, None] * inv_r[None, :]).astype(np.float32).astype(np.float64)
    ang_ss = (pos_s[:, None] * inv_s[None, :]).astype(np.float32).astype(np.float64)
    qdec_s = np.where(valid[:, None], np.exp(lg[None, :] * (i_loc[:, None] - 15.0)), 0.0)
    kdec_s = np.where(valid[:, None], (128.0 ** -0.5) * np.exp(lg[None, :] * (15.0 - i_loc[:, None])), 0.0)
    crs, srs = np.cos(ang_rs), np.sin(ang_rs)
    rt[NTP, :64, 0:256] = (crs[:, None, :] * qdec_s[:, :, None]).reshape(64, 256)
    rt[NTP, :64, 256:512] = (srs[:, None, :] * qdec_s[:, :, None]).reshape(64, 256)
    rt[NTP, :64, 512:768] = (crs[:, None, :] * kdec_s[:, :, None]).reshape(64, 256)
    rt[NTP, :64, 768:1024] = (srs[:, None, :] * kdec_s[:, :, None]).reshape(64, 256)
    rt[NTP, :64, 1024:1032] = np.cos(ang_ss)
    rt[NTP, :64, 1032:1040] = np.sin(ang_ss)
    put("cm", (p[None, :] >= p[:, None]).astype(np.float64))
    cms = (valid[:, None] & valid[None, :] & (b_of[:, None] == b_of[None, :]) & (i_loc[None, :] >= i_loc[:, None]))
    put("cm_s", cms.astype(np.float64))
    put("nb_s", np.where(valid[:, None] & (b_of[:, None] == np.arange(2)[None, :]), 0.0, -30000.0))
    o, w = CT["nhalf"]
    tab[:, o:o + w] = -0.5
    return tab.astype(np.float32), rt.astype(np.float32)


GC128 = [float(np.exp(_lg()[h] * 128.0)) for h in range(4)]
GC16 = [float(np.exp(_lg()[h] * 16.0)) for h in range(4)]
COLG = [(0, 512), (512, 1024), (1024, 1536), (1536, 2048), (2048, 2560), (2560, 2816)]


def build_nc():
    nc = bass.Bass("TRN2", target_bir_lowering=False)

    def din(name, shape):
        return nc.dram_tensor(name, list(shape), F32, kind="ExternalInput").ap()

    def dout(name, shape):
        return nc.dram_tensor(name, list(shape), F32, kind="ExternalOutput").ap()

    xp = din("xp", [SEQ, D])
    xs = din("xs", [2, 16, D])
    ck = din("ck", [2, 128, 128])
    cv = din("cv", [2, 128, 128])
    st_in = din("st", [2, 4, 128, 128])
    w_in = din("w_in", [D, PROJ_W])
    w_out = din("w_out", [D, D])
    w_up = din("w_up", [D, D_FF])
    w_down = din("w_down", [D_FF, D])
    gnw = din("gnw", [1, 512])
    sinks = din("sinks", [1, 8])
    ln1w = din("ln1w", [1, D])
    ln1b = din("ln1b", [1, D])
    ln2w = din("ln2w", [1, D])
    ln2b = din("ln2b", [1, D])
    ctab_d = din("ctab", [128, NCT])
    rtab_d = din("rtab", [NT, 128, RTW])
    ident_d = din("ident", [128, 128])

    y_p = dout("y_p", [SEQ, D])
    y_s = dout("y_s", [2, 16, D])
    k_p = dout("k_p", [128, 128])
    v_p = dout("v_p", [128, 128])
    ret_p = dout("ret_p", [4, 128, 128])
    k_s = dout("k_s", [2, 16, 128])
    v_s = dout("v_s", [2, 16, 128])
    ret_s = dout("ret_s", [2, 4, 128, 128])

    P = Prog()
    finals = []

    def I(eng, meth, *args, reads=(), writes=(), dma=False, **kw):
        return P.op(eng, meth, args, kw, reads=reads, writes=writes, dma=dma)

    def DMA(eng, out, in_, reads=(), writes=()):
        return P.op(eng, "dma_start", (), {"out": out, "in_": in_}, reads=reads, writes=writes, dma=True)

    es = contextlib.ExitStack()
    with es:
        def sb(name, shape, dt, stack=None):
            return (stack or es).enter_context(nc.sbuf_tensor(name, list(shape), dt))

        WA = sb("WA", [128, 8 * PROJ_W], BF16)
        Win = WA[:, :].rearrange("p (k e) -> p k e", k=8)
        WB = sb("WB", [128, 8192], BF16)
        Wo = WB[:, :].rearrange("p (k e) -> p k e", k=8)
        yacc = sb("yacc", [128, NT, D], F32)
        ctab = sb("ctab_sb", [128, NCT], F32)
        ident = sb("ident_sb", [128, 128], BF16)
        identf = sb("identf_sb", [128, 128], F32)
        lnw_b = sb("lnw_b", [128, D], F32)
        lnb_b = sb("lnb_b", [128, D], F32)
        lst = sb("lst", [128, 2, 6], F32)
        lmv = sb("lmv", [128, 2], F32)
        lve = sb("lve", [128, 1], F32)
        lrs = sb("lrs", [128, 1], F32)
        lnm = sb("lnm", [128, 1], F32)
        epsc = sb("epsc", [128, 2], F32)
        epsg, epsl = epsc[:, 0:1], epsc[:, 1:2]
        lnd = sb("lnd", [1, 2], F32)
        gsc = sb("gsc", [1, 2], F32)
        WIN_ALL = ["win_%d_%d" % (kc, c0) for (c0, c1) in COLG for kc in range(8)]
        WO_ALL = ["wo_%d" % kc for kc in range(8)]

        def C(name, rows=128):
            o, w = CT[name]
            return ctab[0:rows, o:o + w]

        PB = [es.enter_context(nc.psum_tensor("pb%d" % i, [128, 512], F32)) for i in range(8)]
        PBb = [PB[i][:, :].bitcast(BF16) for i in range(8)]
        RB = ["pb%d" % i for i in range(8)]

        es1 = contextlib.ExitStack()

        def sb1(name, shape, dt):
            return sb(name, shape, dt, es1)

        zo = sb1("zo", [1, 128], BF16)
        gnw_b = sb1("gnw_b", [128, 512], F32)
        sink_f = sb1("sink_f", [1, 8], F32)
        esink16 = sb1("esink16", [1, 128], BF16)
        rtab = [sb1("rtab_sb%d" % i, [128, RTW], F32) for i in range(2)]
        xf = [sb1("xf%d" % i, [128, D], F32) for i in range(2)]
        xT = sb1("xT", [128, 8, 128], BF16)
        rt = [sb1("rt%d" % i, [128, 4, 64], F32) for i in range(4)]
        RT = ["rt0", "rt1", "rt2", "rt3"]
        q_tm = sb1("q_tm", [128, 512], BF16)
        k_tm = [sb1("k_tm%d" % i, [128, 512], BF16) for i in range(2)]
        v_tm = [sb1("v_tm%d" % i, [128, 512], BF16) for i in range(2)]
        gw = [sb1("gw%d" % i, [128, 512], F32) for i in range(2)]
        sq_tm = sb1("sq_tm", [128, 512], BF16)
        skf = sb1("skf", [128, 128], F32)
        svf = sb1("svf", [128, 128], F32)
        sk_tm = sb1("sk_tm", [128, 128], BF16)
        srt = [rt[i][:, :, :].rearrange("p h i -> p (h i)")[:, 0:64].rearrange("p (h i) -> p h i", i=8) for i in range(4)]
        SRT = RT
        qT = [sb1("qT%d" % i, [128, 4, 128], BF16) for i in range(2)]
        kT = [sb1("kT%d" % i, [128, 4, 128], BF16) for i in range(2)]
        sqT = [sb1("sqT%d" % i, [64, 8, 128], BF16) for i in range(2)]
        skT = [sb1("skT%d" % i, [64, 2, 128], BF16) for i in range(3)]
        svcx = [sb1("svcx%d" % i, [64, 2, 128], BF16) for i in range(6)]
        scT = sb1("scT", [128, 4, 128], BF16)
        S = sb1("S", [128, 4, 128], F32)
        Sg = sb1("Sg", [128, 4, 128], BF16)
        gst = sb1("gst", [128, 4, 6], F32)
        gmv = sb1("gmv", [128, 4, 2], F32)
        gve = sb1("gve", [128, 4], F32)
        grs = sb1("grs", [128, 4], F32)
        on = sb1("on", [128, 512], F32)
        mix_tm = sb1("mix_tm", [128, 512], BF16)
        mixT2 = [sb1("mixT%d" % i, [128, 4, 128], BF16) for i in range(2)]
        PT = sb1("PT", [64, 3, 256], BF16)
        PT2 = sb1("PT2", [64, 3, 256], BF16)
        rden = sb1("rden", [128, 512], F32)
        esink = rden[0:1, 0:256].bitcast(BF16)
        swaT2 = [sb1("swaT%d" % i, [128, 4, 128], BF16) for i in range(2)]
        S_s1 = sb1("S_s1", [128, 4, 128], F32)
        S_s = [S, S_s1]
        Sg_s1 = sb1("Sg_s1", [128, 4, 128], BF16)
        Sg_s = [Sg, Sg_s1]
        qTm = [sb1("qTm%d" % b, [128, 4, 64], BF16) for b in range(2)]
        ckb = sb1("ckb", [128, 128], BF16)
        cvx = [sb1("cvx%d" % b, [128, 2, 128], BF16) for b in range(2)]
        SVS = (2 * NTP) % 6
        svx_s = svcx[SVS]
        ckT = [PT[:, b, :].rearrange("p (g n) -> p g n", g=2) for b in range(2)]
        PTn = PT[:, 2, 0:64]
        PTc = scT[:, 0, 0:64]
        print("sbuf remaining after phase-1 alloc:", nc.sbuf_bytes_remaining)

        DMA("sp", ctab[:, :], ctab_d, writes=["ctab"])
        DMA("pool", ident[:, :], ident_d, writes=["ident"])
        DMA("sp", identf[:, :], ident_d, writes=["identf"])
        DMA("sp", xf[0][:, :], xp[0:128, :], writes=["xf0"])
        DMA("sp", rtab[0][:, :], rtab_d[0], writes=["rtab0"])
        for (c0, c1) in COLG:
            for kc in range(8):
                DMA("pool", Win[:, kc, c0:c1], w_in[kc * 128:(kc + 1) * 128, c0:c1], writes=["win_%d_%d" % (kc, c0)])
        DMA("sp", gnw_b[:, :], gnw.broadcast_to([128, 512]), writes=["gnw_b"])
        DMA("sp", lnw_b[:, :], ln1w.broadcast_to([128, D]), writes=["lnw_b"])
        DMA("sp", lnb_b[:, :], ln1b.broadcast_to([128, D]), writes=["lnb_b"])
        DMA("sp", sink_f[:, :], sinks, writes=["sink_f"])
        for kc in range(8):
            DMA("pool", Wo[:, kc, :], w_out[kc * 128:(kc + 1) * 128, :], writes=["wo_%d" % kc])
        I("act", "activation", sink_f[:, :], sink_f[:, :], AF.Exp, reads=["sink_f"], writes=["sink_f"])
        I("dve", "tensor_copy", esink[0:1, :].rearrange("p (h q) -> p h q", q=64),
          sink_f[0:1, :].unsqueeze(2).broadcast_to([1, 8, 64]), reads=["sink_f"], writes=["esink"])
        I("dve", "tensor_copy", esink16[0:1, :].rearrange("p (h q) -> p h q", q=16),
          sink_f[0:1, :].unsqueeze(2).broadcast_to([1, 8, 16]), reads=["sink_f"], writes=["esink16"])
        I("pool", "memset", epsc[:, 0:1], GN_EPS, writes=["epsc"])
        I("pool", "memset", epsc[:, 1:2], LN_EPS, writes=["epsc"])
        I("pool", "memset", lnd[:, :], 1.0, writes=["lnd"])
        I("pool", "memset", zo[0:1, 0:64], 0.0, writes=["zo"])
        I("pool", "memset", zo[0:1, 64:128], 1.0, writes=["zo"])
        I("pool", "memset", S[:, :, :], 0.0, writes=["S"])
        for i in range(2):
            I("pool", "memset", swaT2[i][:, :, :], 0.0, writes=["swaT%d" % i])
        for i in range(6):
            I("pool", "memset", svcx[i][:, :, :], 1.0, writes=["svcx%d" % i])
        for b in range(2):
            I("pool", "memset", cvx[b][:, :, :], 1.0, writes=["cvx%d" % b])

        def front(t):
            smp = (t == NTP)
            Rr = 64 if smp else 128
            p = t % 2
            rtb, rtn = rtab[p], "rtab%d" % p
            idR = ident[0:Rr, 0:Rr]
            xfp, xfn = xf[p], "xf%d" % p
            for rnd in range(2):
                for k4 in range(4):
                    kc = rnd * 4 + k4
                    I("pe", "transpose", PB[rnd][:, k4 * 128:k4 * 128 + Rr], xfp[0:Rr, kc * 128:(kc + 1) * 128], identf[0:Rr, 0:Rr],
                      reads=[xfn, "identf"], writes=[RB[rnd]])
            for rnd in range(2):
                I("act", "copy", xT[:, rnd * 4:(rnd + 1) * 4, 0:Rr], PB[rnd][:, :].rearrange("p (k n) -> p k n", k=4)[:, :, 0:Rr],
                  reads=[RB[rnd]], writes=["xT"])
            I("act", "mul", yacc[0:Rr, t, :], xfp[0:Rr, :], ALPHA, reads=[xfn], writes=["yacc%d" % t])
            tn = t + 1
            if tn < NTP:
                DMA("sp", xf[tn % 2][:, :], xp[tn * 128:(tn + 1) * 128, :], writes=["xf%d" % (tn % 2)])
            elif tn == NTP:
                I("pool", "memset", xf[tn % 2][0:64, :], 0.0, writes=["xf%d" % (tn % 2)])
                for b in range(2):
                    DMA("sp", xf[tn % 2][b * 32:b * 32 + 16, :], xs[b], writes=["xf%d" % (tn % 2)])
            if tn < NT:
                DMA("sp", rtab[tn % 2][:, :], rtab_d[tn], writes=["rtab%d" % (tn % 2)])
            yield

            def proj(gi, bank):
                c0, c1 = COLG[gi]
                for kc in range(8):
                    I("pe", "matmul", PB[bank][0:Rr, 0:c1 - c0], lhsT=xT[:, kc, 0:Rr], rhs=Win[:, kc, c0:c1],
                      start=(kc == 0), stop=(kc == 7), reads=["xT", "win_%d_%d" % (kc, c0)], writes=[RB[bank]])

            def rope_ret(bank, toff, dst, dname):
                ps4 = PB[bank][0:Rr, :].rearrange("p (h two i) -> p h two i", h=4, two=2)
                dst4 = dst[0:Rr, :].rearrange("p (h two i) -> p h two i", h=4, two=2)
                cb = rtb[0:Rr, toff:toff + 256].rearrange("p (h i) -> p h i", h=4)
                sbb = rtb[0:Rr, toff + 256:toff + 512].rearrange("p (h i) -> p h i", h=4)
                x1, x2 = ps4[:, :, 0, :], ps4[:, :, 1, :]
                I("dve", "tensor_tensor", rt[0][0:Rr], x1, cb, op=ALU.mult, reads=[RB[bank], rtn], writes=[RT[0]])
                I("dve", "tensor_tensor", rt[1][0:Rr], x2, sbb, op=ALU.mult, reads=[RB[bank], rtn], writes=[RT[1]])
                I("dve", "tensor_tensor", rt[2][0:Rr], x2, cb, op=ALU.mult, reads=[RB[bank], rtn], writes=[RT[2]])
                I("dve", "tensor_tensor", rt[3][0:Rr], x1, sbb, op=ALU.mult, reads=[RB[bank], rtn], writes=[RT[3]])
                I("dve", "tensor_tensor", dst4[:, :, 0, :], rt[0][0:Rr], rt[1][0:Rr], op=ALU.subtract,
                  reads=[RT[0], RT[1]], writes=[dname])
                I("dve", "tensor_tensor", dst4[:, :, 1, :], rt[2][0:Rr], rt[3][0:Rr], op=ALU.add,
                  reads=[RT[2], RT[3]], writes=[dname])

            def rope_swa(ps_view, dst_view, nh, bankres, dname):
                cb = rtb[0:Rr, 1024:1032].unsqueeze(1).broadcast_to([Rr, nh, 8])
                sbb = rtb[0:Rr, 1032:1040].unsqueeze(1).broadcast_to([Rr, nh, 8])
                x1, x2 = ps_view[:, :, 0:8], ps_view[:, :, 8:16]
                tt = [srt[i][0:Rr, 0:nh, :] for i in range(4)]
                I("dve", "tensor_tensor", tt[0], x1, cb, op=ALU.mult, reads=[bankres, rtn], writes=[SRT[0]])
                I("dve", "tensor_tensor", tt[1], x2, sbb, op=ALU.mult, reads=[bankres, rtn], writes=[SRT[1]])
                I("dve", "tensor_tensor", tt[2], x2, cb, op=ALU.mult, reads=[bankres, rtn], writes=[SRT[2]])
                I("dve", "tensor_tensor", tt[3], x1, sbb, op=ALU.mult, reads=[bankres, rtn], writes=[SRT[3]])
                I("dve", "tensor_tensor", dst_view[:, :, 0:8], tt[0], tt[1], op=ALU.subtract, reads=[SRT[0], SRT[1]], writes=[dname])
                I("dve", "tensor_tensor", dst_view[:, :, 8:16], tt[2], tt[3], op=ALU.add, reads=[SRT[2], SRT[3]], writes=[dname])

            proj(0, 1)
            rope_ret(1, 0, q_tm, "q_tm")
            yield
            proj(1, 2)
            for h in range(4):
                I("pe", "transpose", PBb[0][:, h * 128:h * 128 + Rr], q_tm[0:Rr, h * 128:(h + 1) * 128], idR,
                  reads=["q_tm", "ident"], writes=[RB[0]])
            rope_ret(2, 512, k_tm[p], "k_tm%d" % p)
            yield
            proj(2, 1)
            for h in range(4):
                I("pe", "transpose", PBb[0][:, 512 + h * 128:512 + h * 128 + Rr], k_tm[p][0:Rr, h * 128:(h + 1) * 128], idR,
                  reads=["k_tm%d" % p, "ident"], writes=[RB[0]])
            I("act", "copy", v_tm[p][0:Rr, :], PB[1][0:Rr, :], reads=[RB[1]], writes=["v_tm%d" % p])
            I("act", "copy", qT[p][:, :, 0:Rr], PBb[0][:, 0:512].rearrange("p (h n) -> p h n", h=4)[:, :, 0:Rr],
              reads=[RB[0]], writes=["qT%d" % p])
            I("act", "copy", kT[p][:, :, 0:Rr], PBb[0][:, 512:1024].rearrange("p (h n) -> p h n", h=4)[:, :, 0:Rr],
              reads=[RB[0]], writes=["kT%d" % p])
            yield
            prev = t - 2
            emb = (prev >= 0) and (prev <= NTP - 2)
            if emb:
                out_proj_half(prev, 0, 0, mm=True, add=False)
            proj(3, 2)
            I("act", "activation", gw[p][0:Rr, :], PB[2][0:Rr, :], AF.Silu, reads=[RB[2]], writes=["gw%d" % p])
            I("act", "activation", lnd[:, 1:2], lnd[:, 0:1], AF.Ln, reads=["lnd"], writes=["lnd2"])
            I("pool", "tensor_tensor", gw[p][0:Rr, :], gw[p][0:Rr, :], gnw_b[0:Rr, :], op=ALU.mult,
              reads=["gw%d" % p, "gnw_b"], writes=["gw%d" % p])
            yield
            if emb:
                out_proj_half(prev, 0, 0, mm=False, add=True)
                out_proj_half(prev, 1, 0, mm=True, add=False)
            proj(4, 1)
            I("act", "copy", sq_tm[0:Rr, :], PB[1][0:Rr, :], reads=[RB[1]], writes=["sq_tm"])
            rope_swa(PB[1][0:Rr, :].rearrange("p (h d) -> p h d", d=64), sq_tm[0:Rr, :].rearrange("p (h d) -> p h d", d=64),
                     8, RB[1], "sq_tm")
            yield
            if emb:
                out_proj_half(prev, 1, 0, mm=False, add=True)
            proj(5, 2)
            need_f32 = smp or (t == NTP - 1)
            I("act", "copy", sk_tm[0:Rr, :], PB[2][0:Rr, 0:128], reads=[RB[2]], writes=["sk_tm"])
            rope_swa(PB[2][0:Rr, 0:128].rearrange("p (h d) -> p h d", d=64), sk_tm[0:Rr, :].rearrange("p (h d) -> p h d", d=64),
                     2, RB[2], "sk_tm")
            if not smp:
                for cc in range(2):
                    c = 2 * t + cc
                    I("act", "copy", svcx[c % 6][:, :, 0:64],
                      PB[2][cc * 64:(cc + 1) * 64, 128:256].rearrange("p (g d) -> p g d", g=2),
                      reads=[RB[2]], writes=["svcx%d" % (c % 6)])
            if need_f32:
                I("act", "copy", skf[0:Rr, :], PB[2][0:Rr, 0:128], reads=[RB[2]], writes=["skf"])
                I("act", "copy", svf[0:Rr, :], PB[2][0:Rr, 128:256], reads=[RB[2]], writes=["svf"])
                skv = skf[0:Rr, :].rearrange("p (h d) -> p h d", d=64)
                I("dve", "tensor_tensor", skv[:, :, 0:8], srt[0][0:Rr, 0:2, :], srt[1][0:Rr, 0:2, :], op=ALU.subtract,
                  reads=[SRT[0], SRT[1]], writes=["skf"])
                I("dve", "tensor_tensor", skv[:, :, 8:16], srt[2][0:Rr, 0:2, :], srt[3][0:Rr, 0:2, :], op=ALU.add,
                  reads=[SRT[2], SRT[3]], writes=["skf"])
                if smp:
                    I("pool", "tensor_copy", svx_s[:, :, 0:64], svf[0:64, :].rearrange("p (g d) -> p g d", g=2),
                      reads=["svf"], writes=["svcx%d" % SVS])
                    for b in range(2):
                        finals.append(DMA("sp", k_s[b], skf[b * 32:b * 32 + 16, :], reads=["skf"]))
                        finals.append(DMA("sp", v_s[b], svf[b * 32:b * 32 + 16, :], reads=["svf"]))
                else:
                    finals.append(DMA("sp", k_p, skf[:, :], reads=["skf"]))
                    finals.append(DMA("sp", v_p, svf[:, :], reads=["svf"]))
            yield
            for h in range(8):
                I("pe", "transpose", PBb[0][0:64, h * 128:h * 128 + Rr], sq_tm[0:Rr, h * 64:(h + 1) * 64], idR,
                  reads=["sq_tm", "ident"], writes=[RB[0]])
            for g in range(2):
                I("pe", "transpose", PBb[1][0:64, g * 128:g * 128 + Rr], sk_tm[0:Rr, g * 64:(g + 1) * 64], idR,
                  reads=["sk_tm", "ident"], writes=[RB[1]])
            I("act", "copy", sqT[p][:, :, 0:Rr], PBb[0][0:64, :].rearrange("p (h n) -> p h n", h=8)[:, :, 0:Rr],
              reads=[RB[0]], writes=["sqT%d" % p])
            I("dve", "tensor_copy", skT[t % 3][:, :, 0:Rr], PBb[1][0:64, 0:256].rearrange("p (g n) -> p g n", g=2)[:, :, 0:Rr],
              reads=[RB[1]], writes=["skT%d" % (t % 3)])
            yield

        def groupnorm_and_mix(Rr, p):
            idR = ident[0:Rr, 0:Rr]
            for h in range(4):
                I("dve", "bn_stats", gst[0:Rr, h, :], PB[4][0:Rr, h * 128:(h + 1) * 128], reads=[RB[4]], writes=["gst"])
            for h in range(4):
                I("dve", "bn_aggr", gmv[0:Rr, h, :], gst[0:Rr, h, :], reads=["gst"], writes=["gmv"])
            I("act", "activation", gve[0:Rr, :], gmv[0:Rr, :, 1], AF.Ln, bias=epsg[0:Rr, :], reads=["gmv", "epsc"], writes=["gve"])
            I("act", "activation", grs[0:Rr, :], gve[0:Rr, :], AF.Exp, scale=-0.5, reads=["gve"], writes=["grs"])
            yield
            for h in range(4):
                I("dve", "tensor_scalar", on[0:Rr, h * 128:(h + 1) * 128], PB[4][0:Rr, h * 128:(h + 1) * 128],
                  gmv[0:Rr, h, 0:1], grs[0:Rr, h:h + 1], op0=ALU.subtract, op1=ALU.mult,
                  reads=[RB[4], "gmv", "grs"], writes=["on"])
            I("dve", "tensor_tensor", mix_tm[0:Rr, :], on[0:Rr, :], gw[p][0:Rr, :], op=ALU.mult,
              reads=["on", "gw%d" % p], writes=["mix_tm"])
            for h in range(4):
                I("pe", "transpose", PBb[3][:, h * 128:h * 128 + Rr], mix_tm[0:Rr, h * 128:(h + 1) * 128], idR,
                  reads=["mix_tm", "ident"], writes=[RB[3]])
            I("act", "copy", mixT2[p][:, :, 0:Rr], PBb[3][:, 0:512].rearrange("p (h n) -> p h n", h=4)[:, :, 0:Rr],
              reads=[RB[3]], writes=["mixT%d" % p])

        def layer_norm(src, sres, Rr, aff="pool", spread=False):
            for hf in range(2):
                I("dve", "bn_stats", lst[0:Rr, hf, :], src[0:Rr, hf * 512:(hf + 1) * 512], reads=sres, writes=["lst"])
            I("dve", "bn_aggr", lmv[0:Rr, :], lst[0:Rr, :, :], reads=["lst"], writes=["lmv"])
            I("act", "activation", lve[0:Rr, :], lmv[0:Rr, 1:2], AF.Ln, bias=epsl[0:Rr, :], reads=["lmv", "epsc"], writes=["lve"])
            I("act", "activation", lrs[0:Rr, :], lve[0:Rr, :], AF.Exp, scale=-0.5, reads=["lve"], writes=["lrs"])
            yield
            if spread:
                I("act", "activation", lve[0:Rr, :], lmv[0:Rr, 0:1], AF.Identity, scale=lrs[0:Rr, :], reads=["lmv", "lrs"], writes=["lve"])
                I("act", "mul", lnm[0:Rr, :], lve[0:Rr, :], -1.0, reads=["lve"], writes=["lnm"])
                I("act", "activation", src[0:Rr, :], src[0:Rr, :], AF.Identity, bias=lnm[0:Rr, :], scale=lrs[0:Rr, :],
                  reads=sres + ["lnm", "lrs"], writes=sres)
                I("pool", "tensor_tensor", src[0:Rr, :], src[0:Rr, :], lnw_b[0:Rr, :], op=ALU.mult, reads=sres + ["lnw_b"], writes=sres)
                I("pool", "tensor_tensor", src[0:Rr, :], src[0:Rr, :], lnb_b[0:Rr, :], op=ALU.add, reads=sres + ["lnb_b"], writes=sres)
                return
            I("dve", "tensor_scalar", src[0:Rr, :], src[0:Rr, :], lmv[0:Rr, 0:1], lrs[0:Rr, :], op0=ALU.subtract, op1=ALU.mult,
              reads=sres + ["lmv", "lrs"], writes=sres)
            I(aff, "tensor_tensor", src[0:Rr, :], src[0:Rr, :], lnw_b[0:Rr, :], op=ALU.mult, reads=sres + ["lnw_b"], writes=sres)
            I(aff, "tensor_tensor", src[0:Rr, :], src[0:Rr, :], lnb_b[0:Rr, :], op=ALU.add, reads=sres + ["lnb_b"], writes=sres)

        def out_proj_half(t, hf, bank, mm=True, add=True):
            Rr = 64 if t == NTP else 128
            pp = t % 2
            if mm:
                for kc in range(8):
                    lhsT = mixT2[pp][:, kc, 0:Rr] if kc < 4 else swaT2[pp][:, kc - 4, 0:Rr]
                    I("pe", "matmul", PB[bank][0:Rr, :], lhsT=lhsT, rhs=Wo[:, kc, hf * 512:(hf + 1) * 512],
                      start=(kc == 0), stop=(kc == 7), reads=[("mixT%d" if kc < 4 else "swaT%d") % pp, "wo_%d" % kc], writes=[RB[bank]])
            if add:
                ysl = yacc[0:Rr, t, hf * 512:(hf + 1) * 512]
                I("dve", "tensor_tensor", ysl, ysl, PB[bank][0:Rr, :], op=ALU.add, reads=[RB[bank], "yacc%d" % t], writes=["yacc%d" % t])

        def ln1_only(t):
            Rr = 64 if t == NTP else 128
            for _ in layer_norm(yacc[:, t, :], ["yacc%d" % t], Rr, aff="dve"):
                yield

        def out_proj_ln1(t):
            Rr = 64 if t == NTP else 128
            for hf in range(2):
                bank = 1 + hf
                for kc in range(8):
                    lhsT = mixT2[t % 2][:, kc, 0:Rr] if kc < 4 else swaT2[t % 2][:, kc - 4, 0:Rr]
                    I("pe", "matmul", PB[bank][0:Rr, :], lhsT=lhsT, rhs=Wo[:, kc, hf * 512:(hf + 1) * 512],
                      start=(kc == 0), stop=(kc == 7), reads=[("mixT%d" if kc < 4 else "swaT%d") % (t % 2), "wo_%d" % kc], writes=[RB[bank]])
            yield
            for hf in range(2):
                ysl = yacc[0:Rr, t, hf * 512:(hf + 1) * 512]
                I("dve", "tensor_tensor", ysl, ysl, PB[1 + hf][0:Rr, :], op=ALU.add, reads=[RB[1 + hf], "yacc%d" % t], writes=["yacc%d" % t])
            for _ in layer_norm(yacc[:, t, :], ["yacc%d" % t], Rr, aff="dve"):
                yield

        def swa_chain(t, p, g, cc, bufs):
            sc_a, sc_b, sc_b_off, PTb, PTn_ = bufs
            c = 2 * t + cc
            js = [j for j in (c - 2, c - 1, c) if j >= 0]
            nj = len(js)
            rhs_q = sqT[p][:, g * 4:(g + 1) * 4, cc * 64:(cc + 1) * 64]
            slots = [PB[sc_a][0:64, 0:256], PB[sc_a][0:64, 256:512], PB[sc_b][0:64, sc_b_off:sc_b_off + 256]]
            for idx, j in enumerate(js):
                tj, jh = j // 2, j % 2
                I("pe", "matmul", slots[idx], lhsT=skT[tj % 3][:, g, jh * 64:(jh + 1) * 64], rhs=rhs_q,
                  start=True, stop=True, reads=["skT%d" % (tj % 3), "sqT%d" % p], writes=[RB[sc_a] if idx < 2 else RB[sc_b]])
            yield
            n6 = min(nj, 2)
            I("act", "activation", PTb[:, 0:n6, :], PB[sc_a][0:64, 0:n6 * 256].rearrange("p (s n) -> p s n", s=n6),
              AF.Exp, scale=0.125, reads=[RB[sc_a]], writes=[PTn_])
            if nj == 3:
                I("act", "activation", PTb[:, 2, :], slots[2], AF.Exp, scale=0.125, reads=[RB[sc_b]], writes=[PTn_])
            yield
            pv = PB[sc_a][:, 0:256]
            for idx, j in enumerate(js):
                I("pe", "matmul", pv, lhsT=svcx[j % 6][:, g, :], rhs=PTb[:, idx, :], start=(idx == 0), stop=False,
                  reads=["svcx%d" % (j % 6), PTn_], writes=[RB[sc_a]])
            I("pe", "matmul", pv, lhsT=zo[0:1, :], rhs=esink[0:1, g * 256:(g + 1) * 256], start=False, stop=True,
              reads=["zo", "esink"], writes=[RB[sc_a]])
            yield
            rd = rden[64:128, g * 256:(g + 1) * 256]
            I("act", "activation", rd, PB[sc_a][64:128, 0:256], AF.Ln, reads=[RB[sc_a]], writes=["rden%d" % g])
            I("act", "activation", rd, rd, AF.Exp, scale=-1.0, reads=["rden%d" % g], writes=["rden%d" % g])
            yield
            numv = PB[sc_a][0:64, 0:256].rearrange("p (r2 par q) -> p r2 par q", r2=2, par=2)
            rdv = rd.rearrange("p (r2 par q) -> p r2 par q", r2=2, par=2)
            for par in range(2):
                I("dve", "tensor_tensor", swaT2[p][par * 64:(par + 1) * 64, g * 2:(g + 1) * 2, cc * 64:(cc + 1) * 64],
                  numv[:, :, par, :], rdv[:, :, par, :], op=ALU.mult, reads=[RB[sc_a], "rden%d" % g], writes=["swaT%d" % p])
            yield

        def ret_gen(t):
            p = t % 2
            for h in range(4):
                I("pe", "matmul", PB[3][:, h * 128:(h + 1) * 128], lhsT=kT[p][:, h, :], rhs=qT[p][:, h, :], start=True, stop=True,
                  reads=["kT%d" % p, "qT%d" % p], writes=[RB[3]])
            yield
            I("dve", "tensor_tensor", scT[:, :, :], PB[3][:, :].rearrange("p (h n) -> p h n", h=4),
              C("cm").unsqueeze(1).broadcast_to([128, 4, 128]), op=ALU.mult, reads=[RB[3], "ctab"], writes=["scT"])
            yield
            for h in range(4):
                I("pe", "matmul", PB[4][:, h * 128:(h + 1) * 128], lhsT=scT[:, h, :], rhs=v_tm[p][:, h * 128:(h + 1) * 128],
                  start=True, stop=(t == 0), reads=["scT", "v_tm%d" % p], writes=[RB[4]])
                if t > 0:
                    I("pe", "matmul", PB[4][:, h * 128:(h + 1) * 128], lhsT=qT[p][:, h, :], rhs=Sg[:, h, :],
                      start=False, stop=True, reads=["qT%d" % p, "Sg"], writes=[RB[4]])
            for h in range(4):
                I("pe", "matmul", PB[3][:, h * 128:(h + 1) * 128], lhsT=k_tm[p][:, h * 128:(h + 1) * 128],
                  rhs=v_tm[p][:, h * 128:(h + 1) * 128], start=True, stop=True, reads=["k_tm%d" % p, "v_tm%d" % p], writes=[RB[3]])
            yield
            for h in range(4):
                I("dve", "scalar_tensor_tensor", S[:, h, :], S[:, h, :], GC128[h], PB[3][:, h * 128:(h + 1) * 128],
                  op0=ALU.mult, op1=ALU.add, reads=[RB[3], "S"], writes=["S"])
            if t < NTP - 1:
                for h in range(4):
                    I("act", "mul", Sg[:, h, :], S[:, h, :], GC128[h], reads=["S"], writes=["Sg"])
            else:
                finals.append(DMA("sp", ret_p.rearrange("h d e -> d h e"), S[:, :, :], reads=["S"]))
            for _ in groupnorm_and_mix(128, p):
                yield
            yield

        def swa_gen(t):
            p = t % 2
            bufs0 = (5, 6, 0, PT, "PT")
            bufs1 = (7, 6, 256, PT2, "PT2")
            for cc in range(2):
                ch = [swa_chain(t, p, 0, cc, bufs0), swa_chain(t, p, 1, cc, bufs1)]
                while ch:
                    for gch in list(ch):
                        try:
                            next(gch)
                        except StopIteration:
                            ch.remove(gch)
                    yield

        def back_prompt(t):
            gens = [ret_gen(t), swa_gen(t)]
            while gens:
                for gg in list(gens):
                    try:
                        next(gg)
                    except StopIteration:
                        gens.remove(gg)
                yield

        def back_sample():
            t = NTP
            p = t % 2
            for b in range(2):
                DMA("sp", S_s[b][:, :, :], st_in[b].rearrange("h d e -> d h e"), writes=["S" if b == 0 else "S_s1"])
                for h in range(4):
                    I("act", "mul", Sg_s[b][:, h, :], S_s[b][:, h, :], GC16[h], reads=["S" if b == 0 else "S_s1"],
                      writes=["Sg" if b == 0 else "Sg_s1"])
                I("pool", "memset", qTm[b][:, :, :], 0.0, writes=["qTm%d" % b])
                I("pool", "tensor_copy", qTm[b][:, :, b * 32:b * 32 + 16], qT[p][:, :, b * 32:b * 32 + 16],
                  reads=["qT%d" % p], writes=["qTm%d" % b])
            for h in range(4):
                I("pe", "matmul", PB[3][0:64, h * 128:h * 128 + 64], lhsT=kT[p][:, h, 0:64], rhs=qT[p][:, h, 0:64], start=True, stop=True,
                  reads=["kT%d" % p, "qT%d" % p], writes=[RB[3]])
            I("dve", "tensor_tensor", scT[0:64, :, 0:64], PB[3][0:64, :].rearrange("p (h n) -> p h n", h=4)[:, :, 0:64],
              C("cm_s", 64).unsqueeze(1).broadcast_to([64, 4, 64]), op=ALU.mult, reads=[RB[3], "ctab"], writes=["scT"])
            yield
            for h in range(4):
                I("pe", "matmul", PB[4][0:64, h * 128:(h + 1) * 128], lhsT=scT[0:64, h, 0:64], rhs=v_tm[p][0:64, h * 128:(h + 1) * 128],
                  start=True, stop=False, reads=["scT", "v_tm%d" % p], writes=[RB[4]])
                for b in range(2):
                    I("pe", "matmul", PB[4][0:64, h * 128:(h + 1) * 128], lhsT=qTm[b][:, h, :], rhs=Sg_s[b][:, h, :],
                      start=False, stop=(b == 1), reads=["qTm%d" % b, "Sg" if b == 0 else "Sg_s1"], writes=[RB[4]])
            for b in range(2):
                sres = "S" if b == 0 else "S_s1"
                for h in range(4):
                    I("pe", "matmul", PB[3][:, h * 128:(h + 1) * 128], lhsT=k_tm[p][b * 32:b * 32 + 16, h * 128:(h + 1) * 128],
                      rhs=v_tm[p][b * 32:b * 32 + 16, h * 128:(h + 1) * 128], start=True, stop=True,
                      reads=["k_tm%d" % p, "v_tm%d" % p], writes=[RB[3]])
                for h in range(4):
                    I("dve", "scalar_tensor_tensor", S_s[b][:, h, :], S_s[b][:, h, :], GC16[h], PB[3][:, h * 128:(h + 1) * 128],
                      op0=ALU.mult, op1=ALU.add, reads=[RB[3], sres], writes=[sres])
                finals.append(DMA("sp", ret_s[b].rearrange("h d e -> d h e"), S_s[b][:, :, :], reads=[sres]))
            yield
            for _ in groupnorm_and_mix(64, p):
                yield
            yield
            nb = C("nb_s", 64)
            for b in range(2):
                DMA("pool", ckb[:, :], ck[b], writes=["ckb"])
                DMA("pool", cvx[b][:, :, 0:64], cv[b].rearrange("k (g d) -> k g d", g=2), writes=["cvx%d" % b])
                for g in range(2):
                    I("pe", "transpose", PBb[5][0:64, g * 128:(g + 1) * 128], ckb[:, g * 64:(g + 1) * 64], ident[:, :],
                      reads=["ckb", "ident"], writes=[RB[5]])
                I("act", "copy", ckT[b], PBb[5][0:64, 0:256].rearrange("p (g n) -> p g n", g=2),
                  reads=[RB[5]], writes=["PT"])
            skT_t = skT[t % 3]
            for b in range(2):
                for g in range(2):
                    rhs_q = sqT[p][:, g * 4:(g + 1) * 4, b * 32:b * 32 + 16]
                    I("pe", "matmul", PB[5][:, 0:64], lhsT=ckT[b][:, g, :], rhs=rhs_q, start=True, stop=True,
                      reads=["PT", "sqT%d" % p], writes=[RB[5]])
                    I("pe", "matmul", PB[6][0:64, 0:64], lhsT=skT_t[:, g, 0:64], rhs=rhs_q, start=True, stop=True,
                      reads=["skT%d" % (t % 3), "sqT%d" % p], writes=[RB[6]])
                    I("act", "activation", PTc, PB[5][:, 0:64], AF.Exp, scale=0.125, reads=[RB[5]], writes=["scT"])
                    I("act", "activation", PTn, PB[6][0:64, 0:64], AF.Exp, bias=nb[:, b:b + 1], scale=0.125,
                      reads=[RB[6], "ctab"], writes=["PT"])
                    pv = PB[7][:, 0:64]
                    I("pe", "matmul", pv, lhsT=cvx[b][:, g, :], rhs=PTc, start=True, stop=False,
                      reads=["cvx%d" % b, "scT"], writes=[RB[7]])
                    I("pe", "matmul", pv, lhsT=svx_s[:, g, :], rhs=PTn, start=False, stop=False,
                      reads=["svcx%d" % SVS, "PT"], writes=[RB[7]])
                    I("pe", "matmul", pv, lhsT=zo[0:1, :], rhs=esink16[0:1, g * 64:(g + 1) * 64], start=False, stop=True,
                      reads=["zo", "esink16"], writes=[RB[7]])
                    I("act", "activation", rden[64:128, 0:64], PB[7][64:128, 0:64], AF.Ln, reads=[RB[7]], writes=["rden0"])
                    I("act", "activation", rden[64:128, 0:64], rden[64:128, 0:64], AF.Exp, scale=-1.0, reads=["rden0"], writes=["rden0"])
                    numv = PB[7][0:64, 0:64].rearrange("p (r2 par q) -> p r2 par q", r2=2, par=2)
                    rdv = rden[64:128, 0:64].rearrange("p (r2 par q) -> p r2 par q", r2=2, par=2)
                    for par in range(2):
                        I("dve", "tensor_tensor", swaT2[p][par * 64:(par + 1) * 64, g * 2:(g + 1) * 2, b * 32:b * 32 + 16],
                          numv[:, :, par, :], rdv[:, :, par, :], op=ALU.mult, reads=[RB[7], "rden0"], writes=["swaT%d" % p])
                yield

        def run_interleaved(gens, ratio=1):
            gens = [g for g in gens if g is not None]
            k = 0
            while gens:
                for gi, g in enumerate(list(gens)):
                    if gi == 0 and len(gens) > 1 and (k % ratio) != 0:
                        continue
                    try:
                        next(g)
                    except StopIteration:
                        gens.remove(g)
                k += 1

        def chain_gens(*gs):
            for g in gs:
                if g is not None:
                    for _ in g:
                        yield

        run_interleaved([front(0)])
        for t in range(NT):
            if t + 1 < NT:
                tl = None
                if t >= 1:
                    tl = ln1_only(t - 1) if (t - 1) <= NTP - 2 else out_proj_ln1(t - 1)
                a = chain_gens(front(t + 1), tl)
            else:
                a = chain_gens(out_proj_ln1(t - 1))
            bk = back_prompt(t) if t < NTP else back_sample()
            run_interleaved([a, bk])
        run_interleaved([out_proj_ln1(NT - 1)])

        def wup(bi, kc):
            if bi == 0:
                return WA[:, kc * 1024:(kc + 1) * 1024]
            if kc < 6:
                return WA[:, 16384 + kc * 1024:16384 + (kc + 1) * 1024]
            return yacc_spill[kc - 6]

        def wdn(bi, fc):
            if bi == 0:
                return WA[:, 8192 + fc * 1024:8192 + (fc + 1) * 1024]
            return WB[:, fc * 1024:(fc + 1) * 1024]

        def load_mlp_weights(grp, extra_w=()):
            bi = grp % 2
            gate = []
            if extra_w:
                I("pool", "memset", gsc[:, :], 0.0, reads=[], writes=list(extra_w) + ["wgate%d" % grp])
                gate = ["wgate%d" % grp]
            for kc in range(8):
                DMA("pool", wup(bi, kc), w_up[kc * 128:(kc + 1) * 128, grp * 1024:(grp + 1) * 1024],
                    reads=gate, writes=["wup%d_%d" % (bi, kc)])
            for fc in range(8):
                DMA("pool", wdn(bi, fc), w_down[grp * 1024 + fc * 128:grp * 1024 + (fc + 1) * 128, :],
                    reads=gate, writes=["wdn%d_%d" % (bi, fc)])

        load_mlp_weights(0, extra_w=WIN_ALL)
        P.barrier(skip=("wup", "wdn", "wgate"))
        es1.close()

        NCOL = NTP * 128 + 64
        x1T = sb("x1T", [128, 8, NCOL], BF16)
        hT = sb("hT", [128, 8, 512], BF16)
        hr = [sb("hr%d" % i, [128, 512], BF16) for i in range(2)]
        x1bf = [sb("x1bf%d" % i, [128, D], BF16) for i in range(2)]
        wsp = sb("wsp", [128, 2048], BF16)
        yacc_spill = [wsp[:, 0:1024], wsp[:, 1024:2048]]
        print("sbuf remaining after phase-2 alloc:", nc.sbuf_bytes_remaining)

        DMA("sp", lnw_b[:, :], ln2w.broadcast_to([128, D]), writes=["lnw_b"])
        DMA("sp", lnb_b[:, :], ln2b.broadcast_to([128, D]), writes=["lnb_b"])

        blocks = [(NTP * 128, 64, [NTP])]
        for i in range(0, NTP, 4):
            tl = list(range(i, min(i + 4, NTP)))
            blocks.append((i * 128, 128 * len(tl), tl))
        nbuild = [0]

        def build_x1T(tiles):
            for t in tiles:
                Rr = 64 if t == NTP else 128
                k = nbuild[0] % 2
                nbuild[0] += 1
                bank = 0 if k == 0 else 7
                I("act", "copy", x1bf[k][0:Rr, :], yacc[0:Rr, t, :], reads=["yacc%d" % t], writes=["x1bf%d" % k])
                for kc in range(8):
                    I("pe", "transpose", PBb[bank][:, kc * 128:kc * 128 + Rr], x1bf[k][0:Rr, kc * 128:(kc + 1) * 128],
                      ident[0:Rr, 0:Rr], reads=["x1bf%d" % k, "ident"], writes=[RB[bank]])
                I("dve", "tensor_copy", x1T[:, :, t * 128:t * 128 + Rr],
                  PBb[bank][:, :].rearrange("p (k n) -> p k n", k=8)[:, :, 0:Rr], reads=[RB[bank]], writes=["x1T%d" % t])

        def epilogue(t):
            Rr = 64 if t == NTP else 128
            for _ in layer_norm(yacc[:, t, :], ["yacc%d" % t], Rr, aff="dve"):
                pass
            if t < NTP:
                finals.append(DMA("sp", y_p[t * 128:(t + 1) * 128, :], yacc[:, t, :], reads=["yacc%d" % t]))
            else:
                for b in range(2):
                    finals.append(DMA("sp", y_s[b], yacc[b * 32:b * 32 + 16, t, :], reads=["yacc%d" % t]))

        nhr = [0]
        pending = []

        def mlp_pass(grp, bidx, final_block):
            bi = grp % 2
            c0, ncol, tiles = blocks[bidx]
            if grp == 0 and bidx == 0:
                build_x1T(tiles)
            for fc in range(8):
                bank = 1 + (fc % 2)
                for kc in range(8):
                    I("pe", "matmul", PB[bank][:, 0:ncol], lhsT=wup(bi, kc)[:, fc * 128:(fc + 1) * 128], rhs=x1T[:, kc, c0:c0 + ncol],
                      start=(kc == 0), stop=(kc == 7), reads=["wup%d_%d" % (bi, kc)] + ["x1T%d" % tt for tt in tiles],
                      writes=[RB[bank]])
                hrb = hr[nhr[0] % 2]
                hrn = "hr%d" % (nhr[0] % 2)
                nhr[0] += 1
                I("act", "activation", hrb[:, 0:ncol], PB[bank][:, 0:ncol], AF.Relu, reads=[RB[bank]], writes=[hrn])
                I("pool", "tensor_tensor", hT[:, fc, 0:ncol], hrb[:, 0:ncol], hrb[:, 0:ncol], op=ALU.mult,
                  reads=[hrn], writes=["hT%d" % fc])
                if pending and fc % 2 == 1:
                    epilogue(pending.pop(0))
            if grp == 0 and bidx + 1 < len(blocks):
                build_x1T(blocks[bidx + 1][2])
            if grp == 0 and bidx == 0:
                load_mlp_weights(1, extra_w=WO_ALL)
            for ti, t in enumerate(tiles):
                Rr = 64 if t == NTP else 128
                for hf in range(2):
                    bank = 3 + 2 * (ti % 2) + hf
                    for fc in range(8):
                        I("pe", "matmul", PB[bank][0:Rr, :], lhsT=hT[:, fc, ti * 128:ti * 128 + Rr],
                          rhs=wdn(bi, fc)[:, hf * 512:(hf + 1) * 512], start=(fc == 0), stop=(fc == 7),
                          reads=["hT%d" % fc, "wdn%d_%d" % (bi, fc)], writes=[RB[bank]])
                    ysl = yacc[0:Rr, t, hf * 512:(hf + 1) * 512]
                    if grp == 0:
                        I("dve", "scalar_tensor_tensor", ysl, ysl, ALPHA, PB[bank][0:Rr, :], op0=ALU.mult, op1=ALU.add,
                          reads=[RB[bank], "yacc%d" % t], writes=["yacc%d" % t])
                    else:
                        I("dve", "tensor_tensor", ysl, ysl, PB[bank][0:Rr, :], op=ALU.add,
                          reads=[RB[bank], "yacc%d" % t], writes=["yacc%d" % t])
                if grp == 3 and final_block:
                    while pending:
                        epilogue(pending.pop(0))
                    pending.append(t)
            if grp == 3 and not final_block:
                pending.extend(tiles)

        nb = len(blocks)
        for bidx in range(nb):
            mlp_pass(0, bidx, False)
        load_mlp_weights(2)
        for bidx in range(nb):
            mlp_pass(1, bidx, False)
        load_mlp_weights(3)
        lead = min(3, nb)
        for bidx in range(lead):
            mlp_pass(2, bidx, False)
        for bidx in range(lead):
            mlp_pass(3, bidx, bidx == nb - 1)
        for bidx in range(lead, nb):
            mlp_pass(2, bidx, False)
            mlp_pass(3, bidx, bidx == nb - 1)
        while pending:
            epilogue(pending.pop(0))

        P.emit(nc, finals)
        print("ops per engine:", P.stats)
    return nc


_NC_CACHE = {}


def kernel(x_prompt, x_sample, cache_swa_k, cache_swa_v, state_ret, w_in, ret_gn_w, swa_sinks,
           w_out, ln1_w, ln1_b, w_up, w_down, ln2_w, ln2_b):
    f = lambda a: np.ascontiguousarray(np.asarray(a, dtype=np.float32))
    x_prompt, x_sample = f(x_prompt), f(x_sample)
    cache_swa_k, cache_swa_v, state_ret = f(cache_swa_k), f(cache_swa_v), f(state_ret)
    if "nc" not in _NC_CACHE:
        _NC_CACHE["nc"] = build_nc()
    nc = _NC_CACHE["nc"]
    ctab, rtab = make_tables()
    ident = np.eye(128, dtype=np.float32)
    shared = {
        "w_in": f(w_in[0]), "w_out": f(w_out[0]), "w_up": f(w_up[0]), "w_down": f(w_down[0]),
        "gnw": f(ret_gn_w[0]).reshape(1, 512), "sinks": f(swa_sinks[0]).reshape(1, 8),
        "ln1w": f(ln1_w[0]).reshape(1, D), "ln1b": f(ln1_b[0]).reshape(1, D),
        "ln2w": f(ln2_w[0]).reshape(1, D), "ln2b": f(ln2_b[0]).reshape(1, D),
        "ctab": ctab, "rtab": rtab, "ident": ident,
    }
    in_maps = []
    for c in range(8):
        m = dict(shared)
        m["xp"] = x_prompt[c]
        m["xs"] = x_sample[2 * c:2 * c + 2]
        m["ck"] = cache_swa_k[0, 2 * c:2 * c + 2].reshape(2, 128, 128)
        m["cv"] = cache_swa_v[0, 2 * c:2 * c + 2].reshape(2, 128, 128)
        m["st"] = state_ret[0, 2 * c:2 * c + 2]
        in_maps.append(m)
    res = run_bass_kernel_spmd(nc, in_maps, core_ids=list(range(8)))
    rs = res.results
    y_p = np.stack([rs[c]["y_p"] for c in range(8)], axis=0)
    y_s = np.concatenate([rs[c]["y_s"] for c in range(8)], axis=0)
    k_p = np.stack([rs[c]["k_p"].reshape(128, 2, 64) for c in range(8)], axis=0)[None]
    v_p = np.stack([rs[c]["v_p"].reshape(128, 2, 64) for c in range(8)], axis=0)[None]
    r_p = np.stack([rs[c]["ret_p"] for c in range(8)], axis=0)[None]
    k_s = np.concatenate([rs[c]["k_s"].reshape(2, 16, 2, 64) for c in range(8)], axis=0)[None]
    v_s = np.concatenate([rs[c]["v_s"].reshape(2, 16, 2, 64) for c in range(8)], axis=0)[None]
    r_s = np.concatenate([rs[c]["ret_s"] for c in range(8)], axis=0)[None]
    return tuple(np.ascontiguousarray(a, dtype=np.float32) for a in (y_p, y_s, k_p, v_p, r_p, k_s, v_s, r_s))
```

```python
import contextlib
import numpy as np
import concourse.bass as bass
import concourse.mybir as mybir
from concourse.bass_utils import run_bass_kernel_spmd

F32 = mybir.dt.float32
BF16 = mybir.dt.bfloat16
AF = mybir.ActivationFunctionType
ALU = mybir.AluOpType

ENGS = ("pe", "act", "dve", "pool", "sp")

D = 1024
SEQ = 2048
NTP = 16
NT = 17
PROJ_W = 2816
D_FF = 4096
ALPHA = 2.0 ** 0.25
LN_EPS = 1e-5
GN_EPS = 1e-5
PAST = 2048


class Op:
    __slots__ = ("eng", "meth", "args", "kw", "deps", "is_dma", "needs_inc", "sem", "val")

    def __init__(self, eng, meth, args, kw, is_dma):
        self.eng = eng
        self.meth = meth
        self.args = args
        self.kw = kw
        self.deps = []
        self.is_dma = is_dma
        self.needs_inc = is_dma
        self.sem = None
        self.val = None


class Res:
    __slots__ = ("name", "last_w", "readers")

    def __init__(self, name):
        self.name = name
        self.last_w = None
        self.readers = []


class Prog:
    def __init__(self, n_dma_sems=64):
        self.ops = []
        self.n_dma_sems = n_dma_sems
        self.res = {}
        self.barrier_op = None

    def R(self, name):
        r = self.res.get(name)
        if r is None:
            r = Res(name)
            r.last_w = self.barrier_op
            self.res[name] = r
        return r

    def op(self, eng, meth, args, kw, reads=(), writes=(), dma=False):
        o = Op(eng, meth, args, kw, dma)
        reads = [self.R(r) if isinstance(r, str) else r for r in reads]
        writes = [self.R(w) if isinstance(w, str) else w for w in writes]
        deps = []
        for r in reads:
            if r.last_w is not None:
                deps.append(r.last_w)
            if r.name.startswith("pb"):
                deps.extend(x for x in r.readers if x.eng != eng)
        for w in writes:
            if w.last_w is not None:
                deps.append(w.last_w)
            deps.extend(w.readers)
        for r in reads:
            r.readers.append(o)
        for w in writes:
            w.last_w = o
            w.readers = []
        seen = set()
        for d in deps:
            if d is o or id(d) in seen:
                continue
            if (not d.is_dma) and (not dma) and d.eng == "pe" and eng == "pe":
                continue
            seen.add(id(d))
            o.deps.append(d)
            d.needs_inc = True
        self.ops.append(o)
        return o

    def barrier(self, skip=()):
        o = self.op("sp", "nop", (), {"nofuse": True}, reads=(),
                    writes=[r for r in self.res.values() if not r.name.startswith(tuple(skip))])
        self.barrier_op = o
        o.needs_inc = True
        return o

    def emit(self, nc, finals):
        import os as _os
        _tr = int(_os.environ.get("KTRUNC", "0"))
        if _tr:
            keep = set(id(o) for o in self.ops[:_tr])
            self.ops = self.ops[:_tr]
            finals = [f for f in finals if id(f) in keep]
            print("TRUNCATED to", _tr, "ops; last:", self.ops[-1].eng, self.ops[-1].meth)
        ops = self.ops
        st = contextlib.ExitStack()
        eng_sem = {e: st.enter_context(nc.semaphore("s_" + e)) for e in ENGS}
        dma_sems = [st.enter_context(nc.semaphore("d%d" % i)) for i in range(self.n_dma_sems)]
        cnt = {e: 0 for e in ENGS}
        dma_use = [0] * self.n_dma_sems
        dma_prev = [None] * self.n_dma_sems
        half = self.n_dma_sems // 2
        kq = {"pool": 0, "hw": 0}
        for o in ops:
            if o.is_dma:
                if o.eng == "pool":
                    s = kq["pool"] % half
                    kq["pool"] += 1
                else:
                    s = half + kq["hw"] % (self.n_dma_sems - half)
                    kq["hw"] += 1
                dma_use[s] += 1
                o.sem = ("d", s)
                o.val = 16 * dma_use[s]
                if dma_prev[s] is not None:
                    o.deps.append(dma_prev[s])
                dma_prev[s] = o
            elif o.needs_inc:
                cnt[o.eng] += 1
                o.sem = ("e", o.eng)
                o.val = cnt[o.eng]

        def semh(key):
            return eng_sem[key[1]] if key[0] == "e" else dma_sems[key[1]]

        vc_of = {}
        eng_known = {e: {} for e in ENGS}
        plans = {e: [] for e in ENGS}
        for o in ops:
            kn = eng_known[o.eng]
            wm = {}
            for d in o.deps:
                key, val = d.sem, d.val
                if kn.get(key, 0) >= val:
                    continue
                if wm.get(key, 0) < val:
                    wm[key] = val
                for k2, v2 in vc_of[id(d)].items():
                    if kn.get(k2, 0) < v2:
                        kn[k2] = v2
            plans[o.eng].append((list(wm.items()), o))
            if o.sem is not None:
                v = dict(kn)
                if v.get(o.sem, 0) < o.val:
                    v[o.sem] = o.val
                vc_of[id(o)] = v
        self.stats = {e: len(plans[e]) for e in ENGS}
        self.stats["waits"] = sum(len(w) for e in ENGS for w, _ in plans[e])

        with st, nc.Block() as block:
            def mk(ename):
                def body(eng):
                    for waits, o in plans[ename]:
                        for key, val in waits:
                            eng.wait_ge(semh(key), val)
                        ins = getattr(eng, o.meth)(*o.args, **o.kw)
                        if o.sem is not None:
                            ins.then_inc(semh(o.sem), 16 if o.is_dma else 1)
                    if ename == "sp":
                        for f in finals:
                            eng.wait_ge(semh(f.sem), f.val)
                return body
            block.tensor(mk("pe"))
            block.scalar(mk("act"))
            block.vector(mk("dve"))
            block.gpsimd(mk("pool"))
            block.sync(mk("sp"))


CT = {}
_off = 0
for _n, _w in [("cm", 128), ("cm_s", 64), ("nb_s", 2), ("nhalf", 4)]:
    CT[_n] = (_off, _w)
    _off += _w
NCT = _off
RTW = 1040


def _lg():
    return np.log1p(-(2.0 ** (-5.0 - np.arange(4, dtype=np.float64))))


def make_tables():
    lg = _lg()
    tab = np.zeros((128, NCT), np.float64)

    def put(name, arr):
        o, w = CT[name]
        arr = np.asarray(arr, np.float64)
        tab[:arr.shape[0], o:o + w] = arr.reshape(arr.shape[0], w)

    p = np.arange(128)
    inv_r = (10000.0 ** (-np.arange(64, dtype=np.float32) / 64)).astype(np.float32)
    inv_s = (500000.0 ** (-np.arange(8, dtype=np.float32) / 8)).astype(np.float32)
    rt = np.zeros((NT, 128, RTW), np.float64)
    pos = (np.arange(NTP)[:, None] * 128 + p[None, :]).astype(np.float32)
    ang_r = (pos[:, :, None] * inv_r[None, None, :]).astype(np.float32).astype(np.float64)
    ang_s = (pos[:, :, None] * inv_s[None, None, :]).astype(np.float32).astype(np.float64)
    qdec = np.exp(lg[None, :] * (p[:, None] - 127.0))
    kdec = (128.0 ** -0.5) * np.exp(lg[None, :] * (127.0 - p[:, None]))
    cr, sr = np.cos(ang_r), np.sin(ang_r)
    rt[:NTP, :, 0:256] = (cr[:, :, None, :] * qdec[None, :, :, None]).reshape(NTP, 128, 256)
    rt[:NTP, :, 256:512] = (sr[:, :, None, :] * qdec[None, :, :, None]).reshape(NTP, 128, 256)
    rt[:NTP, :, 512:768] = (cr[:, :, None, :] * kdec[None, :, :, None]).reshape(NTP, 128, 256)
    rt[:NTP, :, 768:1024] = (sr[:, :, None, :] * kdec[None, :, :, None]).reshape(NTP, 128, 256)
    rt[:NTP, :, 1024:1032] = np.cos(ang_s)
    rt[:NTP, :, 1032:1040] = np.sin(ang_s)
    rows = np.arange(64)
    i_loc = rows % 32
    valid = i_loc < 16
    b_of = rows // 32
    pos_s = (PAST + np.minimum(i_loc, 15)).astype(np.float32)
    ang_rs = (pos_s[:, None] * inv_r[None, :]).astype(np.float32).astype(np.float64)
    ang_ss = (pos_s[:, None] * inv_s[None, :]).astype(np.float32).astype(np.float64)
    qdec_s = np.where(valid[:, None], np.exp(lg[None, :] * (i_loc[:, None] - 15.0)), 0.0)
    kdec_s = np.where(valid[:, None], (128.0 ** -0.5) * np.exp(lg[None, :] * (15.0 - i_loc[:, None])), 0.0)
    crs, srs = np.cos(ang_rs), np.sin(ang_rs)
    rt[NTP, :64, 0:256] = (crs[:, None, :] * qdec_s[:, :, None]).reshape(64, 256)
    rt[NTP, :64, 256:512] = (srs[:, None, :] * qdec_s[:, :, None]).reshape(64, 256)
    rt[NTP, :64, 512:768] = (crs[:, None, :] * kdec_s[:, :, None]).reshape(64, 256)
    rt[NTP, :64, 768:1024] = (srs[:, None, :] * kdec_s[:, :, None]).reshape(64, 256)
    rt[NTP, :64, 1024:1032] = np.cos(ang_ss)
    rt[NTP, :64, 1032:1040] = np.sin(ang_ss)
    put("cm", (p[None, :] >= p[:, None]).astype(np.float64))
    cms = (valid[:, None] & valid[None, :] & (b_of[:, None] == b_of[None, :]) & (i_loc[None, :] >= i_loc[:, None]))
    put("cm_s", cms.astype(np.float64))
    put("nb_s", np.where(valid[:, None] & (b_of[:, None] == np.arange(2)[None, :]), 0.0, -30000.0))
    o, w = CT["nhalf"]
    tab[:, o:o + w] = -0.5
    return tab.astype(np.float32), rt.astype(np.float32)


GC128 = [float(np.exp(_lg()[h] * 128.0)) for h in range(4)]
GC16 = [float(np.exp(_lg()[h] * 16.0)) for h in range(4)]
COLG = [(0, 512), (512, 1024), (1024, 1536), (1536, 2048), (2048, 2560), (2560, 2816)]


def build_nc():
    nc = bass.Bass("TRN2", target_bir_lowering=False)

    def din(name, shape):
        return nc.dram_tensor(name, list(shape), F32, kind="ExternalInput").ap()

    def dout(name, shape):
        return nc.dram_tensor(name, list(shape), F32, kind="ExternalOutput").ap()

    xp = din("xp", [SEQ, D])
    xs = din("xs", [2, 16, D])
    ck = din("ck", [2, 128, 128])
    cv = din("cv", [2, 128, 128])
    st_in = din("st", [2, 4, 128, 128])
    w_in = din("w_in", [D, PROJ_W])
    w_out = din("w_out", [D, D])
    w_up = din("w_up", [D, D_FF])
    w_down = din("w_down", [D_FF, D])
    gnw = din("gnw", [1, 512])
    sinks = din("sinks", [1, 8])
    ln1w = din("ln1w", [1, D])
    ln1b = din("ln1b", [1, D])
    ln2w = din("ln2w", [1, D])
    ln2b = din("ln2b", [1, D])
    ctab_d = din("ctab", [128, NCT])
    rtab_d = din("rtab", [NT, 128, RTW])
    ident_d = din("ident", [128, 128])

    y_p = dout("y_p", [SEQ, D])
    y_s = dout("y_s", [2, 16, D])
    k_p = dout("k_p", [128, 128])
    v_p = dout("v_p", [128, 128])
    ret_p = dout("ret_p", [4, 128, 128])
    k_s = dout("k_s", [2, 16, 128])
    v_s = dout("v_s", [2, 16, 128])
    ret_s = dout("ret_s", [2, 4, 128, 128])

    P = Prog()
    finals = []

    def I(eng, meth, *args, reads=(), writes=(), dma=False, **kw):
        return P.op(eng, meth, args, kw, reads=reads, writes=writes, dma=dma)

    def DMA(eng, out, in_, reads=(), writes=()):
        return P.op(eng, "dma_start", (), {"out": out, "in_": in_}, reads=reads, writes=writes, dma=True)

    es = contextlib.ExitStack()
    with es:
        def sb(name, shape, dt, stack=None):
            return (stack or es).enter_context(nc.sbuf_tensor(name, list(shape), dt))

        WA = sb("WA", [128, 8 * PROJ_W], BF16)
        Win = WA[:, :].rearrange("p (k e) -> p k e", k=8)
        WB = sb("WB", [128, 8192], BF16)
        Wo = WB[:, :].rearrange("p (k e) -> p k e", k=8)
        yacc = sb("yacc", [128, NT, D], F32)
        ctab = sb("ctab_sb", [128, NCT], F32)
        ident = sb("ident_sb", [128, 128], BF16)
        identf = sb("identf_sb", [128, 128], F32)
        lnw_b = sb("lnw_b", [128, D], F32)
        lnb_b = sb("lnb_b", [128, D], F32)
        lst = sb("lst", [128, 2, 6], F32)
        lmv = sb("lmv", [128, 2], F32)
        lve = sb("lve", [128, 1], F32)
        lrs = sb("lrs", [128, 1], F32)
        lnm = sb("lnm", [128, 1], F32)
        epsc = sb("epsc", [128, 2], F32)
        epsg, epsl = epsc[:, 0:1], epsc[:, 1:2]
        lnd = sb("lnd", [1, 2], F32)
        gsc = sb("gsc", [1, 2], F32)
        WIN_ALL = ["win_%d_%d" % (kc, c0) for (c0, c1) in COLG for kc in range(8)]
        WO_ALL = ["wo_%d" % kc for kc in range(8)]

        def C(name, rows=128):
            o, w = CT[name]
            return ctab[0:rows, o:o + w]

        PB = [es.enter_context(nc.psum_tensor("pb%d" % i, [128, 512], F32)) for i in range(8)]
        PBb = [PB[i][:, :].bitcast(BF16) for i in range(8)]
        RB = ["pb%d" % i for i in range(8)]

        es1 = contextlib.ExitStack()

        def sb1(name, shape, dt):
            return sb(name, shape, dt, es1)

        zo = sb1("zo", [1, 128], BF16)
        gnw_b = sb1("gnw_b", [128, 512], F32)
        sink_f = sb1("sink_f", [1, 8], F32)
        esink16 = sb1("esink16", [1, 128], BF16)
        rtab = [sb1("rtab_sb%d" % i, [128, RTW], F32) for i in range(2)]
        xf = [sb1("xf%d" % i, [128, D], F32) for i in range(2)]
        xT = sb1("xT", [128, 8, 128], BF16)
        rt = [sb1("rt%d" % i, [128, 4, 64], F32) for i in range(4)]
        RT = ["rt0", "rt1", "rt2", "rt3"]
        q_tm = sb1("q_tm", [128, 512], BF16)
        k_tm = [sb1("k_tm%d" % i, [128, 512], BF16) for i in range(2)]
        v_tm = [sb1("v_tm%d" % i, [128, 512], BF16) for i in range(2)]
        gw = [sb1("gw%d" % i, [128, 512], F32) for i in range(2)]
        sq_tm = sb1("sq_tm", [128, 512], BF16)
        skf = sb1("skf", [128, 128], F32)
        svf = sb1("svf", [128, 128], F32)
        sk_tm = sb1("sk_tm", [128, 128], BF16)
        srt = [rt[i][:, :, :].rearrange("p h i -> p (h i)")[:, 0:64].rearrange("p (h i) -> p h i", i=8) for i in range(4)]
        SRT = RT
        qT = [sb1("qT%d" % i, [128, 4, 128], BF16) for i in range(2)]
        kT = [sb1("kT%d" % i, [128, 4, 128], BF16) for i in range(2)]
        sqT = [sb1("sqT%d" % i, [64, 8, 128], BF16) for i in range(2)]
        skT = [sb1("skT%d" % i, [64, 2, 128], BF16) for i in range(3)]
        svcx = [sb1("svcx%d" % i, [64, 2, 128], BF16) for i in range(6)]
        scT = sb1("scT", [128, 4, 128], BF16)
        S = sb1("S", [128, 4, 128], F32)
        Sg = sb1("Sg", [128, 4, 128], BF16)
        gst = sb1("gst", [128, 4, 6], F32)
        gmv = sb1("gmv", [128, 4, 2], F32)
        gve = sb1("gve", [128, 4], F32)
        grs = sb1("grs", [128, 4], F32)
        on = sb1("on", [128, 512], F32)
        mix_tm = sb1("mix_tm", [128, 512], BF16)
        mixT2 = [sb1("mixT%d" % i, [128, 4, 128], BF16) for i in range(2)]
        PT = sb1("PT", [64, 3, 256], BF16)
        PT2 = sb1("PT2", [64, 3, 256], BF16)
        rden = sb1("rden", [128, 512], F32)
        esink = rden[0:1, 0:256].bitcast(BF16)
        swaT2 = [sb1("swaT%d" % i, [128, 4, 128], BF16) for i in range(2)]
        S_s1 = sb1("S_s1", [128, 4, 128], F32)
        S_s = [S, S_s1]
        Sg_s1 = sb1("Sg_s1", [128, 4, 128], BF16)
        Sg_s = [Sg, Sg_s1]
        qTm = [sb1("qTm%d" % b, [128, 4, 64], BF16) for b in range(2)]
        ckb = sb1("ckb", [128, 128], BF16)
        cvx = [sb1("cvx%d" % b, [128, 2, 128], BF16) for b in range(2)]
        SVS = (2 * NTP) % 6
        svx_s = svcx[SVS]
        ckT = [PT[:, b, :].rearrange("p (g n) -> p g n", g=2) for b in range(2)]
        PTn = PT[:, 2, 0:64]
        PTc = scT[:, 0, 0:64]
        print("sbuf remaining after phase-1 alloc:", nc.sbuf_bytes_remaining)

        DMA("sp", ctab[:, :], ctab_d, writes=["ctab"])
        DMA("pool", ident[:, :], ident_d, writes=["ident"])
        DMA("sp", identf[:, :], ident_d, writes=["identf"])
        DMA("sp", xf[0][:, :], xp[0:128, :], writes=["xf0"])
        DMA("sp", rtab[0][:, :], rtab_d[0], writes=["rtab0"])
        for (c0, c1) in COLG:
            for kc in range(8):
                DMA("pool", Win[:, kc, c0:c1], w_in[kc * 128:(kc + 1) * 128, c0:c1], writes=["win_%d_%d" % (kc, c0)])
        DMA("sp", gnw_b[:, :], gnw.broadcast_to([128, 512]), writes=["gnw_b"])
        DMA("sp", lnw_b[:, :], ln1w.broadcast_to([128, D]), writes=["lnw_b"])
        DMA("sp", lnb_b[:, :], ln1b.broadcast_to([128, D]), writes=["lnb_b"])
        DMA("sp", sink_f[:, :], sinks, writes=["sink_f"])
        for kc in range(8):
            DMA("pool", Wo[:, kc, :], w_out[kc * 128:(kc + 1) * 128, :], writes=["wo_%d" % kc])
        I("act", "activation", sink_f[:, :], sink_f[:, :], AF.Exp, reads=["sink_f"], writes=["sink_f"])
        I("dve", "tensor_copy", esink[0:1, :].rearrange("p (h q) -> p h q", q=64),
          sink_f[0:1, :].unsqueeze(2).broadcast_to([1, 8, 64]), reads=["sink_f"], writes=["esink"])
        I("dve", "tensor_copy", esink16[0:1, :].rearrange("p (h q) -> p h q", q=16),
          sink_f[0:1, :].unsqueeze(2).broadcast_to([1, 8, 16]), reads=["sink_f"], writes=["esink16"])
        I("pool", "memset", epsc[:, 0:1], GN_EPS, writes=["epsc"])
        I("pool", "memset", epsc[:, 1:2], LN_EPS, writes=["epsc"])
        I("pool", "memset", lnd[:, :], 1.0, writes=["lnd"])
        I("pool", "memset", zo[0:1, 0:64], 0.0, writes=["zo"])
        I("pool", "memset", zo[0:1, 64:128], 1.0, writes=["zo"])
        I("pool", "memset", S[:, :, :], 0.0, writes=["S"])
        for i in range(2):
            I("pool", "memset", swaT2[i][:, :, :], 0.0, writes=["swaT%d" % i])
        for i in range(6):
            I("pool", "memset", svcx[i][:, :, :], 1.0, writes=["svcx%d" % i])
        for b in range(2):
            I("pool", "memset", cvx[b][:, :, :], 1.0, writes=["cvx%d" % b])

        def front(t):
            smp = (t == NTP)
            Rr = 64 if smp else 128
            p = t % 2
            rtb, rtn = rtab[p], "rtab%d" % p
            idR = ident[0:Rr, 0:Rr]
            xfp, xfn = xf[p], "xf%d" % p
            for rnd in range(2):
                for k4 in range(4):
                    kc = rnd * 4 + k4
                    I("pe", "transpose", PB[0][:, k4 * 128:k4 * 128 + Rr], xfp[0:Rr, kc * 128:(kc + 1) * 128], identf[0:Rr, 0:Rr],
                      reads=[xfn, "identf"], writes=[RB[0]])
                I("act", "copy", xT[:, rnd * 4:(rnd + 1) * 4, 0:Rr], PB[0][:, :].rearrange("p (k n) -> p k n", k=4)[:, :, 0:Rr],
                  reads=[RB[0]], writes=["xT"])
            I("act", "mul", yacc[0:Rr, t, :], xfp[0:Rr, :], ALPHA, reads=[xfn], writes=["yacc%d" % t])
            tn = t + 1
            if tn < NTP:
                DMA("sp", xf[tn % 2][:, :], xp[tn * 128:(tn + 1) * 128, :], writes=["xf%d" % (tn % 2)])
            elif tn == NTP:
                I("pool", "memset", xf[tn % 2][0:64, :], 0.0, writes=["xf%d" % (tn % 2)])
                for b in range(2):
                    DMA("sp", xf[tn % 2][b * 32:b * 32 + 16, :], xs[b], writes=["xf%d" % (tn % 2)])
            if tn < NT:
                DMA("sp", rtab[tn % 2][:, :], rtab_d[tn], writes=["rtab%d" % (tn % 2)])
            yield

            def proj(gi, bank):
                c0, c1 = COLG[gi]
                for kc in range(8):
                    I("pe", "matmul", PB[bank][0:Rr, 0:c1 - c0], lhsT=xT[:, kc, 0:Rr], rhs=Win[:, kc, c0:c1],
                      start=(kc == 0), stop=(kc == 7), reads=["xT", "win_%d_%d" % (kc, c0)], writes=[RB[bank]])

            def rope_ret(bank, toff, dst, dname):
                ps4 = PB[bank][0:Rr, :].rearrange("p (h two i) -> p h two i", h=4, two=2)
                dst4 = dst[0:Rr, :].rearrange("p (h two i) -> p h two i", h=4, two=2)
                cb = rtb[0:Rr, toff:toff + 256].rearrange("p (h i) -> p h i", h=4)
                sbb = rtb[0:Rr, toff + 256:toff + 512].rearrange("p (h i) -> p h i", h=4)
                x1, x2 = ps4[:, :, 0, :], ps4[:, :, 1, :]
                I("dve", "tensor_tensor", rt[0][0:Rr], x1, cb, op=ALU.mult, reads=[RB[bank], rtn], writes=[RT[0]])
                I("dve", "tensor_tensor", rt[1][0:Rr], x2, sbb, op=ALU.mult, reads=[RB[bank], rtn], writes=[RT[1]])
                I("dve", "tensor_tensor", rt[2][0:Rr], x2, cb, op=ALU.mult, reads=[RB[bank], rtn], writes=[RT[2]])
                I("dve", "tensor_tensor", rt[3][0:Rr], x1, sbb, op=ALU.mult, reads=[RB[bank], rtn], writes=[RT[3]])
                I("dve", "tensor_tensor", dst4[:, :, 0, :], rt[0][0:Rr], rt[1][0:Rr], op=ALU.subtract,
                  reads=[RT[0], RT[1]], writes=[dname])
                I("dve", "tensor_tensor", dst4[:, :, 1, :], rt[2][0:Rr], rt[3][0:Rr], op=ALU.add,
                  reads=[RT[2], RT[3]], writes=[dname])

            def rope_swa(ps_view, dst_view, nh, bankres, dname):
                cb = rtb[0:Rr, 1024:1032].unsqueeze(1).broadcast_to([Rr, nh, 8])
                sbb = rtb[0:Rr, 1032:1040].unsqueeze(1).broadcast_to([Rr, nh, 8])
                x1, x2 = ps_view[:, :, 0:8], ps_view[:, :, 8:16]
                tt = [srt[i][0:Rr, 0:nh, :] for i in range(4)]
                I("dve", "tensor_tensor", tt[0], x1, cb, op=ALU.mult, reads=[bankres, rtn], writes=[SRT[0]])
                I("dve", "tensor_tensor", tt[1], x2, sbb, op=ALU.mult, reads=[bankres, rtn], writes=[SRT[1]])
                I("dve", "tensor_tensor", tt[2], x2, cb, op=ALU.mult, reads=[bankres, rtn], writes=[SRT[2]])
                I("dve", "tensor_tensor", tt[3], x1, sbb, op=ALU.mult, reads=[bankres, rtn], writes=[SRT[3]])
                I("dve", "tensor_tensor", dst_view[:, :, 0:8], tt[0], tt[1], op=ALU.subtract, reads=[SRT[0], SRT[1]], writes=[dname])
                I("dve", "tensor_tensor", dst_view[:, :, 8:16], tt[2], tt[3], op=ALU.add, reads=[SRT[2], SRT[3]], writes=[dname])

            proj(0, 1)
            rope_ret(1, 0, q_tm, "q_tm")
            yield
            proj(1, 2)
            for h in range(4):
                I("pe", "transpose", PBb[0][:, h * 128:h * 128 + Rr], q_tm[0:Rr, h * 128:(h + 1) * 128], idR,
                  reads=["q_tm", "ident"], writes=[RB[0]])
            rope_ret(2, 512, k_tm[p], "k_tm%d" % p)
            yield
            proj(2, 1)
            for h in range(4):
                I("pe", "transpose", PBb[0][:, 512 + h * 128:512 + h * 128 + Rr], k_tm[p][0:Rr, h * 128:(h + 1) * 128], idR,
                  reads=["k_tm%d" % p, "ident"], writes=[RB[0]])
            I("act", "copy", v_tm[p][0:Rr, :], PB[1][0:Rr, :], reads=[RB[1]], writes=["v_tm%d" % p])
            I("act", "copy", qT[p][:, :, 0:Rr], PBb[0][:, 0:512].rearrange("p (h n) -> p h n", h=4)[:, :, 0:Rr],
              reads=[RB[0]], writes=["qT%d" % p])
            I("act", "copy", kT[p][:, :, 0:Rr], PBb[0][:, 512:1024].rearrange("p (h n) -> p h n", h=4)[:, :, 0:Rr],
              reads=[RB[0]], writes=["kT%d" % p])
            yield
            prev = t - 2
            emb = (prev >= 0) and (prev <= NTP - 2)
            if emb:
                out_proj_half(prev, 0, 0, mm=True, add=False)
            proj(3, 2)
            I("act", "activation", gw[p][0:Rr, :], PB[2][0:Rr, :], AF.Silu, reads=[RB[2]], writes=["gw%d" % p])
            I("act", "activation", lnd[:, 1:2], lnd[:, 0:1], AF.Ln, reads=["lnd"], writes=["lnd2"])
            I("pool", "tensor_tensor", gw[p][0:Rr, :], gw[p][0:Rr, :], gnw_b[0:Rr, :], op=ALU.mult,
              reads=["gw%d" % p, "gnw_b"], writes=["gw%d" % p])
            yield
            if emb:
                out_proj_half(prev, 0, 0, mm=False, add=True)
                out_proj_half(prev, 1, 0, mm=True, add=False)
            proj(4, 1)
            I("act", "copy", sq_tm[0:Rr, :], PB[1][0:Rr, :], reads=[RB[1]], writes=["sq_tm"])
            rope_swa(PB[1][0:Rr, :].rearrange("p (h d) -> p h d", d=64), sq_tm[0:Rr, :].rearrange("p (h d) -> p h d", d=64),
                     8, RB[1], "sq_tm")
            yield
            if emb:
                out_proj_half(prev, 1, 0, mm=False, add=True)
            proj(5, 2)
            need_f32 = smp or (t == NTP - 1)
            I("act", "copy", sk_tm[0:Rr, :], PB[2][0:Rr, 0:128], reads=[RB[2]], writes=["sk_tm"])
            rope_swa(PB[2][0:Rr, 0:128].rearrange("p (h d) -> p h d", d=64), sk_tm[0:Rr, :].rearrange("p (h d) -> p h d", d=64),
                     2, RB[2], "sk_tm")
            if not smp:
                for cc in range(2):
                    c = 2 * t + cc
                    I("act", "copy", svcx[c % 6][:, :, 0:64],
                      PB[2][cc * 64:(cc + 1) * 64, 128:256].rearrange("p (g d) -> p g d", g=2),
                      reads=[RB[2]], writes=["svcx%d" % (c % 6)])
            if need_f32:
                I("act", "copy", skf[0:Rr, :], PB[2][0:Rr, 0:128], reads=[RB[2]], writes=["skf"])
                I("act", "copy", svf[0:Rr, :], PB[2][0:Rr, 128:256], reads=[RB[2]], writes=["svf"])
                skv = skf[0:Rr, :].rearrange("p (h d) -> p h d", d=64)
                I("dve", "tensor_tensor", skv[:, :, 0:8], srt[0][0:Rr, 0:2, :], srt[1][0:Rr, 0:2, :], op=ALU.subtract,
                  reads=[SRT[0], SRT[1]], writes=["skf"])
                I("dve", "tensor_tensor", skv[:, :, 8:16], srt[2][0:Rr, 0:2, :], srt[3][0:Rr, 0:2, :], op=ALU.add,
                  reads=[SRT[2], SRT[3]], writes=["skf"])
                if smp:
                    I("pool", "tensor_copy", svx_s[:, :, 0:64], svf[0:64, :].rearrange("p (g d) -> p g d", g=2),
                      reads=["svf"], writes=["svcx%d" % SVS])
                    for b in range(2):
                        finals.append(DMA("sp", k_s[b], skf[b * 32:b * 32 + 16, :], reads=["skf"]))
                        finals.append(DMA("sp", v_s[b], svf[b * 32:b * 32 + 16, :], reads=["svf"]))
                else:
                    finals.append(DMA("sp", k_p, skf[:, :], reads=["skf"]))
                    finals.append(DMA("sp", v_p, svf[:, :], reads=["svf"]))
            yield
            for h in range(8):
                I("pe", "transpose", PBb[0][0:64, h * 128:h * 128 + Rr], sq_tm[0:Rr, h * 64:(h + 1) * 64], idR,
                  reads=["sq_tm", "ident"], writes=[RB[0]])
            for g in range(2):
                I("pe", "transpose", PBb[1][0:64, g * 128:g * 128 + Rr], sk_tm[0:Rr, g * 64:(g + 1) * 64], idR,
                  reads=["sk_tm", "ident"], writes=[RB[1]])
            I("act", "copy", sqT[p][:, :, 0:Rr], PBb[0][0:64, :].rearrange("p (h n) -> p h n", h=8)[:, :, 0:Rr],
              reads=[RB[0]], writes=["sqT%d" % p])
            I("dve", "tensor_copy", skT[t % 3][:, :, 0:Rr], PBb[1][0:64, 0:256].rearrange("p (g n) -> p g n", g=2)[:, :, 0:Rr],
              reads=[RB[1]], writes=["skT%d" % (t % 3)])
            yield

        def groupnorm_and_mix(Rr, p):
            idR = ident[0:Rr, 0:Rr]
            for h in range(4):
                I("dve", "bn_stats", gst[0:Rr, h, :], PB[4][0:Rr, h * 128:(h + 1) * 128], reads=[RB[4]], writes=["gst"])
            for h in range(4):
                I("dve", "bn_aggr", gmv[0:Rr, h, :], gst[0:Rr, h, :], reads=["gst"], writes=["gmv"])
            I("act", "activation", gve[0:Rr, :], gmv[0:Rr, :, 1], AF.Ln, bias=epsg[0:Rr, :], reads=["gmv", "epsc"], writes=["gve"])
            I("act", "activation", grs[0:Rr, :], gve[0:Rr, :], AF.Exp, scale=-0.5, reads=["gve"], writes=["grs"])
            yield
            for h in range(4):
                I("dve", "tensor_scalar", on[0:Rr, h * 128:(h + 1) * 128], PB[4][0:Rr, h * 128:(h + 1) * 128],
                  gmv[0:Rr, h, 0:1], grs[0:Rr, h:h + 1], op0=ALU.subtract, op1=ALU.mult,
                  reads=[RB[4], "gmv", "grs"], writes=["on"])
            I("dve", "tensor_tensor", mix_tm[0:Rr, :], on[0:Rr, :], gw[p][0:Rr, :], op=ALU.mult,
              reads=["on", "gw%d" % p], writes=["mix_tm"])
            for h in range(4):
                I("pe", "transpose", PBb[3][:, h * 128:h * 128 + Rr], mix_tm[0:Rr, h * 128:(h + 1) * 128], idR,
                  reads=["mix_tm", "ident"], writes=[RB[3]])
            I("act", "copy", mixT2[p][:, :, 0:Rr], PBb[3][:, 0:512].rearrange("p (h n) -> p h n", h=4)[:, :, 0:Rr],
              reads=[RB[3]], writes=["mixT%d" % p])

        def layer_norm(src, sres, Rr, aff="pool", spread=False):
            for hf in range(2):
                I("dve", "bn_stats", lst[0:Rr, hf, :], src[0:Rr, hf * 512:(hf + 1) * 512], reads=sres, writes=["lst"])
            I("dve", "bn_aggr", lmv[0:Rr, :], lst[0:Rr, :, :], reads=["lst"], writes=["lmv"])
            I("act", "activation", lve[0:Rr, :], lmv[0:Rr, 1:2], AF.Ln, bias=epsl[0:Rr, :], reads=["lmv", "epsc"], writes=["lve"])
            I("act", "activation", lrs[0:Rr, :], lve[0:Rr, :], AF.Exp, scale=-0.5, reads=["lve"], writes=["lrs"])
            yield
            if spread:
                I("act", "activation", lve[0:Rr, :], lmv[0:Rr, 0:1], AF.Identity, scale=lrs[0:Rr, :], reads=["lmv", "lrs"], writes=["lve"])
                I("act", "mul", lnm[0:Rr, :], lve[0:Rr, :], -1.0, reads=["lve"], writes=["lnm"])
                I("act", "activation", src[0:Rr, :], src[0:Rr, :], AF.Identity, bias=lnm[0:Rr, :], scale=lrs[0:Rr, :],
                  reads=sres + ["lnm", "lrs"], writes=sres)
                I("pool", "tensor_tensor", src[0:Rr, :], src[0:Rr, :], lnw_b[0:Rr, :], op=ALU.mult, reads=sres + ["lnw_b"], writes=sres)
                I("pool", "tensor_tensor", src[0:Rr, :], src[0:Rr, :], lnb_b[0:Rr, :], op=ALU.add, reads=sres + ["lnb_b"], writes=sres)
                return
            if aff == "dve":
                I("dve", "scalar_tensor_tensor", src[0:Rr, :], src[0:Rr, :], lmv[0:Rr, 0:1], lnw_b[0:Rr, :],
                  op0=ALU.subtract, op1=ALU.mult, reads=sres + ["lmv", "lnw_b"], writes=sres)
                I("dve", "scalar_tensor_tensor", src[0:Rr, :], src[0:Rr, :], lrs[0:Rr, :], lnb_b[0:Rr, :],
                  op0=ALU.mult, op1=ALU.add, reads=sres + ["lrs", "lnb_b"], writes=sres)
                return
            I("dve", "tensor_scalar", src[0:Rr, :], src[0:Rr, :], lmv[0:Rr, 0:1], lrs[0:Rr, :], op0=ALU.subtract, op1=ALU.mult,
              reads=sres + ["lmv", "lrs"], writes=sres)
            I(aff, "tensor_tensor", src[0:Rr, :], src[0:Rr, :], lnw_b[0:Rr, :], op=ALU.mult, reads=sres + ["lnw_b"], writes=sres)
            I(aff, "tensor_tensor", src[0:Rr, :], src[0:Rr, :], lnb_b[0:Rr, :], op=ALU.add, reads=sres + ["lnb_b"], writes=sres)

        def out_proj_half(t, hf, bank, mm=True, add=True):
            Rr = 64 if t == NTP else 128
            pp = t % 2
            if mm:
                for kc in range(8):
                    lhsT = mixT2[pp][:, kc, 0:Rr] if kc < 4 else swaT2[pp][:, kc - 4, 0:Rr]
                    I("pe", "matmul", PB[bank][0:Rr, :], lhsT=lhsT, rhs=Wo[:, kc, hf * 512:(hf + 1) * 512],
                      start=(kc == 0), stop=(kc == 7), reads=[("mixT%d" if kc < 4 else "swaT%d") % pp, "wo_%d" % kc], writes=[RB[bank]])
            if add:
                ysl = yacc[0:Rr, t, hf * 512:(hf + 1) * 512]
                I("dve", "tensor_tensor", ysl, ysl, PB[bank][0:Rr, :], op=ALU.add, reads=[RB[bank], "yacc%d" % t], writes=["yacc%d" % t])

        def ln1_only(t):
            Rr = 64 if t == NTP else 128
            for _ in layer_norm(yacc[:, t, :], ["yacc%d" % t], Rr, aff="dve"):
                yield

        def out_proj_ln1(t):
            Rr = 64 if t == NTP else 128
            for hf in range(2):
                bank = 1 + hf
                for kc in range(8):
                    lhsT = mixT2[t % 2][:, kc, 0:Rr] if kc < 4 else swaT2[t % 2][:, kc - 4, 0:Rr]
                    I("pe", "matmul", PB[bank][0:Rr, :], lhsT=lhsT, rhs=Wo[:, kc, hf * 512:(hf + 1) * 512],
                      start=(kc == 0), stop=(kc == 7), reads=[("mixT%d" if kc < 4 else "swaT%d") % (t % 2), "wo_%d" % kc], writes=[RB[bank]])
            yield
            for hf in range(2):
                ysl = yacc[0:Rr, t, hf * 512:(hf + 1) * 512]
                I("dve", "tensor_tensor", ysl, ysl, PB[1 + hf][0:Rr, :], op=ALU.add, reads=[RB[1 + hf], "yacc%d" % t], writes=["yacc%d" % t])
            for _ in layer_norm(yacc[:, t, :], ["yacc%d" % t], Rr, aff="dve"):
                yield

        def swa_chain(t, p, g, cc, bufs):
            sc_a, sc_b, sc_b_off, PTb, PTn_ = bufs
            c = 2 * t + cc
            js = [j for j in (c - 2, c - 1, c) if j >= 0]
            nj = len(js)
            rhs_q = sqT[p][:, g * 4:(g + 1) * 4, cc * 64:(cc + 1) * 64]
            slots = [PB[sc_a][0:64, 0:256], PB[sc_a][0:64, 256:512], PB[sc_b][0:64, sc_b_off:sc_b_off + 256]]
            for idx, j in enumerate(js):
                tj, jh = j // 2, j % 2
                I("pe", "matmul", slots[idx], lhsT=skT[tj % 3][:, g, jh * 64:(jh + 1) * 64], rhs=rhs_q,
                  start=True, stop=True, reads=["skT%d" % (tj % 3), "sqT%d" % p], writes=[RB[sc_a] if idx < 2 else RB[sc_b]])
            yield
            n6 = min(nj, 2)
            I("act", "activation", PTb[:, 0:n6, :], PB[sc_a][0:64, 0:n6 * 256].rearrange("p (s n) -> p s n", s=n6),
              AF.Exp, scale=0.125, reads=[RB[sc_a]], writes=[PTn_])
            if nj == 3:
                I("act", "activation", PTb[:, 2, :], slots[2], AF.Exp, scale=0.125, reads=[RB[sc_b]], writes=[PTn_])
            yield
            pv = PB[sc_a][:, 0:256]
            for idx, j in enumerate(js):
                I("pe", "matmul", pv, lhsT=svcx[j % 6][:, g, :], rhs=PTb[:, idx, :], start=(idx == 0), stop=False,
                  reads=["svcx%d" % (j % 6), PTn_], writes=[RB[sc_a]])
            I("pe", "matmul", pv, lhsT=zo[0:1, :], rhs=esink[0:1, g * 256:(g + 1) * 256], start=False, stop=True,
              reads=["zo", "esink"], writes=[RB[sc_a]])
            yield
            rd = rden[64:128, g * 256:(g + 1) * 256]
            I("act", "activation", rd, PB[sc_a][64:128, 0:256], AF.Ln, reads=[RB[sc_a]], writes=["rden%d" % g])
            I("act", "activation", rd, rd, AF.Exp, scale=-1.0, reads=["rden%d" % g], writes=["rden%d" % g])
            yield
            numv = PB[sc_a][0:64, 0:256].rearrange("p (r2 par q) -> p r2 par q", r2=2, par=2)
            rdv = rd.rearrange("p (r2 par q) -> p r2 par q", r2=2, par=2)
            for par in range(2):
                I("dve", "tensor_tensor", swaT2[p][par * 64:(par + 1) * 64, g * 2:(g + 1) * 2, cc * 64:(cc + 1) * 64],
                  numv[:, :, par, :], rdv[:, :, par, :], op=ALU.mult, reads=[RB[sc_a], "rden%d" % g], writes=["swaT%d" % p])
            yield

        def ret_gen(t):
            p = t % 2
            for h in range(4):
                I("pe", "matmul", PB[3][:, h * 128:(h + 1) * 128], lhsT=kT[p][:, h, :], rhs=qT[p][:, h, :], start=True, stop=True,
                  reads=["kT%d" % p, "qT%d" % p], writes=[RB[3]])
            yield
            I("dve", "tensor_tensor", scT[:, :, :], PB[3][:, :].rearrange("p (h n) -> p h n", h=4),
              C("cm").unsqueeze(1).broadcast_to([128, 4, 128]), op=ALU.mult, reads=[RB[3], "ctab"], writes=["scT"])
            yield
            for h in range(4):
                I("pe", "matmul", PB[4][:, h * 128:(h + 1) * 128], lhsT=scT[:, h, :], rhs=v_tm[p][:, h * 128:(h + 1) * 128],
                  start=True, stop=(t == 0), reads=["scT", "v_tm%d" % p], writes=[RB[4]])
                if t > 0:
                    I("pe", "matmul", PB[4][:, h * 128:(h + 1) * 128], lhsT=qT[p][:, h, :], rhs=Sg[:, h, :],
                      start=False, stop=True, reads=["qT%d" % p, "Sg"], writes=[RB[4]])
            for h in range(4):
                I("pe", "matmul", PB[3][:, h * 128:(h + 1) * 128], lhsT=k_tm[p][:, h * 128:(h + 1) * 128],
                  rhs=v_tm[p][:, h * 128:(h + 1) * 128], start=True, stop=True, reads=["k_tm%d" % p, "v_tm%d" % p], writes=[RB[3]])
            yield
            for h in range(4):
                I("dve", "scalar_tensor_tensor", S[:, h, :], S[:, h, :], GC128[h], PB[3][:, h * 128:(h + 1) * 128],
                  op0=ALU.mult, op1=ALU.add, reads=[RB[3], "S"], writes=["S"])
            if t < NTP - 1:
                for h in range(4):
                    I("act", "mul", Sg[:, h, :], S[:, h, :], GC128[h], reads=["S"], writes=["Sg"])
            else:
                finals.append(DMA("sp", ret_p.rearrange("h d e -> d h e"), S[:, :, :], reads=["S"]))
            for _ in groupnorm_and_mix(128, p):
                yield
            yield

        def swa_gen(t):
            p = t % 2
            bufs0 = (5, 6, 0, PT, "PT")
            bufs1 = (7, 6, 256, PT2, "PT2")
            for cc in range(2):
                ch = [swa_chain(t, p, 0, cc, bufs0), swa_chain(t, p, 1, cc, bufs1)]
                while ch:
                    for gch in list(ch):
                        try:
                            next(gch)
                        except StopIteration:
                            ch.remove(gch)
                    yield

        def back_prompt(t):
            gens = [ret_gen(t), swa_gen(t)]
            while gens:
                for gg in list(gens):
                    try:
                        next(gg)
                    except StopIteration:
                        gens.remove(gg)
                yield

        def back_sample():
            t = NTP
            p = t % 2
            for b in range(2):
                DMA("sp", S_s[b][:, :, :], st_in[b].rearrange("h d e -> d h e"), writes=["S" if b == 0 else "S_s1"])
                for h in range(4):
                    I("act", "mul", Sg_s[b][:, h, :], S_s[b][:, h, :], GC16[h], reads=["S" if b == 0 else "S_s1"],
                      writes=["Sg" if b == 0 else "Sg_s1"])
                I("pool", "memset", qTm[b][:, :, :], 0.0, writes=["qTm%d" % b])
                I("pool", "tensor_copy", qTm[b][:, :, b * 32:b * 32 + 16], qT[p][:, :, b * 32:b * 32 + 16],
                  reads=["qT%d" % p], writes=["qTm%d" % b])
            for h in range(4):
                I("pe", "matmul", PB[3][0:64, h * 128:h * 128 + 64], lhsT=kT[p][:, h, 0:64], rhs=qT[p][:, h, 0:64], start=True, stop=True,
                  reads=["kT%d" % p, "qT%d" % p], writes=[RB[3]])
            I("dve", "tensor_tensor", scT[0:64, :, 0:64], PB[3][0:64, :].rearrange("p (h n) -> p h n", h=4)[:, :, 0:64],
              C("cm_s", 64).unsqueeze(1).broadcast_to([64, 4, 64]), op=ALU.mult, reads=[RB[3], "ctab"], writes=["scT"])
            yield
            for h in range(4):
                I("pe", "matmul", PB[4][0:64, h * 128:(h + 1) * 128], lhsT=scT[0:64, h, 0:64], rhs=v_tm[p][0:64, h * 128:(h + 1) * 128],
                  start=True, stop=False, reads=["scT", "v_tm%d" % p], writes=[RB[4]])
                for b in range(2):
                    I("pe", "matmul", PB[4][0:64, h * 128:(h + 1) * 128], lhsT=qTm[b][:, h, :], rhs=Sg_s[b][:, h, :],
                      start=False, stop=(b == 1), reads=["qTm%d" % b, "Sg" if b == 0 else "Sg_s1"], writes=[RB[4]])
            for b in range(2):
                sres = "S" if b == 0 else "S_s1"
                for h in range(4):
                    I("pe", "matmul", PB[3][:, h * 128:(h + 1) * 128], lhsT=k_tm[p][b * 32:b * 32 + 16, h * 128:(h + 1) * 128],
                      rhs=v_tm[p][b * 32:b * 32 + 16, h * 128:(h + 1) * 128], start=True, stop=True,
                      reads=["k_tm%d" % p, "v_tm%d" % p], writes=[RB[3]])
                for h in range(4):
                    I("dve", "scalar_tensor_tensor", S_s[b][:, h, :], S_s[b][:, h, :], GC16[h], PB[3][:, h * 128:(h + 1) * 128],
                      op0=ALU.mult, op1=ALU.add, reads=[RB[3], sres], writes=[sres])
                finals.append(DMA("sp", ret_s[b].rearrange("h d e -> d h e"), S_s[b][:, :, :], reads=[sres]))
            yield
            for _ in groupnorm_and_mix(64, p):
                yield
            yield
            nb = C("nb_s", 64)
            for b in range(2):
                DMA("pool", ckb[:, :], ck[b], writes=["ckb"])
                DMA("pool", cvx[b][:, :, 0:64], cv[b].rearrange("k (g d) -> k g d", g=2), writes=["cvx%d" % b])
                for g in range(2):
                    I("pe", "transpose", PBb[5][0:64, g * 128:(g + 1) * 128], ckb[:, g * 64:(g + 1) * 64], ident[:, :],
                      reads=["ckb", "ident"], writes=[RB[5]])
                I("act", "copy", ckT[b], PBb[5][0:64, 0:256].rearrange("p (g n) -> p g n", g=2),
                  reads=[RB[5]], writes=["PT"])
            skT_t = skT[t % 3]
            for b in range(2):
                for g in range(2):
                    rhs_q = sqT[p][:, g * 4:(g + 1) * 4, b * 32:b * 32 + 16]
                    I("pe", "matmul", PB[5][:, 0:64], lhsT=ckT[b][:, g, :], rhs=rhs_q, start=True, stop=True,
                      reads=["PT", "sqT%d" % p], writes=[RB[5]])
                    I("pe", "matmul", PB[6][0:64, 0:64], lhsT=skT_t[:, g, 0:64], rhs=rhs_q, start=True, stop=True,
                      reads=["skT%d" % (t % 3), "sqT%d" % p], writes=[RB[6]])
                    I("act", "activation", PTc, PB[5][:, 0:64], AF.Exp, scale=0.125, reads=[RB[5]], writes=["scT"])
                    I("act", "activation", PTn, PB[6][0:64, 0:64], AF.Exp, bias=nb[:, b:b + 1], scale=0.125,
                      reads=[RB[6], "ctab"], writes=["PT"])
                    pv = PB[7][:, 0:64]
                    I("pe", "matmul", pv, lhsT=cvx[b][:, g, :], rhs=PTc, start=True, stop=False,
                      reads=["cvx%d" % b, "scT"], writes=[RB[7]])
                    I("pe", "matmul", pv, lhsT=svx_s[:, g, :], rhs=PTn, start=False, stop=False,
                      reads=["svcx%d" % SVS, "PT"], writes=[RB[7]])
                    I("pe", "matmul", pv, lhsT=zo[0:1, :], rhs=esink16[0:1, g * 64:(g + 1) * 64], start=False, stop=True,
                      reads=["zo", "esink16"], writes=[RB[7]])
                    I("act", "activation", rden[64:128, 0:64], PB[7][64:128, 0:64], AF.Ln, reads=[RB[7]], writes=["rden0"])
                    I("act", "activation", rden[64:128, 0:64], rden[64:128, 0:64], AF.Exp, scale=-1.0, reads=["rden0"], writes=["rden0"])
                    numv = PB[7][0:64, 0:64].rearrange("p (r2 par q) -> p r2 par q", r2=2, par=2)
                    rdv = rden[64:128, 0:64].rearrange("p (r2 par q) -> p r2 par q", r2=2, par=2)
                    for par in range(2):
                        I("dve", "tensor_tensor", swaT2[p][par * 64:(par + 1) * 64, g * 2:(g + 1) * 2, b * 32:b * 32 + 16],
                          numv[:, :, par, :], rdv[:, :, par, :], op=ALU.mult, reads=[RB[7], "rden0"], writes=["swaT%d" % p])
                yield

        def run_interleaved(gens, ratio=1):
            gens = [g for g in gens if g is not None]
            k = 0
            while gens:
                for gi, g in enumerate(list(gens)):
                    if gi == 0 and len(gens) > 1 and (k % ratio) != 0:
                        continue
                    try:
                        next(g)
                    except StopIteration:
                        gens.remove(g)
                k += 1

        def chain_gens(*gs):
            for g in gs:
                if g is not None:
                    for _ in g:
                        yield

        run_interleaved([front(0)])
        for t in range(NT):
            if t + 1 < NT:
                tl = None
                if t >= 1:
                    tl = ln1_only(t - 1) if (t - 1) <= NTP - 2 else out_proj_ln1(t - 1)
                a = chain_gens(front(t + 1), tl)
            else:
                a = chain_gens(out_proj_ln1(t - 1))
            bk = back_prompt(t) if t < NTP else back_sample()
            run_interleaved([a, bk])
        run_interleaved([out_proj_ln1(NT - 1)])

        def wup(bi, kc):
            if bi == 0:
                return WA[:, kc * 1024:(kc + 1) * 1024]
            if kc < 6:
                return WA[:, 16384 + kc * 1024:16384 + (kc + 1) * 1024]
            return yacc_spill[kc - 6]

        def wdn(bi, fc):
            if bi == 0:
                return WA[:, 8192 + fc * 1024:8192 + (fc + 1) * 1024]
            return WB[:, fc * 1024:(fc + 1) * 1024]

        def load_mlp_weights(grp, extra_w=()):
            bi = grp % 2
            gate = []
            if extra_w:
                I("pool", "memset", gsc[:, :], 0.0, reads=[], writes=list(extra_w) + ["wgate%d" % grp])
                gate = ["wgate%d" % grp]
            for kc in range(8):
                DMA("pool", wup(bi, kc), w_up[kc * 128:(kc + 1) * 128, grp * 1024:(grp + 1) * 1024],
                    reads=gate, writes=["wup%d_%d" % (bi, kc)])
            for fc in range(8):
                DMA("pool", wdn(bi, fc), w_down[grp * 1024 + fc * 128:grp * 1024 + (fc + 1) * 128, :],
                    reads=gate, writes=["wdn%d_%d" % (bi, fc)])

        load_mlp_weights(0, extra_w=WIN_ALL)
        P.barrier(skip=("wup", "wdn", "wgate"))
        es1.close()

        NCOL = NTP * 128 + 64
        x1T = sb("x1T", [128, 8, NCOL], BF16)
        hT = sb("hT", [128, 8, 512], BF16)
        hr = [sb("hr%d" % i, [128, 512], BF16) for i in range(2)]
        x1bf = [sb("x1bf%d" % i, [128, D], BF16) for i in range(2)]
        wsp = sb("wsp", [128, 2048], BF16)
        yacc_spill = [wsp[:, 0:1024], wsp[:, 1024:2048]]
        print("sbuf remaining after phase-2 alloc:", nc.sbuf_bytes_remaining)

        DMA("sp", lnw_b[:, :], ln2w.broadcast_to([128, D]), writes=["lnw_b"])
        DMA("sp", lnb_b[:, :], ln2b.broadcast_to([128, D]), writes=["lnb_b"])

        blocks = [(NTP * 128, 64, [NTP])]
        for i in range(0, NTP, 4):
            tl = list(range(i, min(i + 4, NTP)))
            blocks.append((i * 128, 128 * len(tl), tl))
        nbuild = [0]

        def build_x1T(tiles):
            for t in tiles:
                Rr = 64 if t == NTP else 128
                k = nbuild[0] % 2
                nbuild[0] += 1
                bank = 0 if k == 0 else 7
                I("act", "copy", x1bf[k][0:Rr, :], yacc[0:Rr, t, :], reads=["yacc%d" % t], writes=["x1bf%d" % k])
                for kc in range(8):
                    I("pe", "transpose", PBb[bank][:, kc * 128:kc * 128 + Rr], x1bf[k][0:Rr, kc * 128:(kc + 1) * 128],
                      ident[0:Rr, 0:Rr], reads=["x1bf%d" % k, "ident"], writes=[RB[bank]])
                I("dve", "tensor_copy", x1T[:, :, t * 128:t * 128 + Rr],
                  PBb[bank][:, :].rearrange("p (k n) -> p k n", k=8)[:, :, 0:Rr], reads=[RB[bank]], writes=["x1T%d" % t])

        def epilogue(t):
            Rr = 64 if t == NTP else 128
            for _ in layer_norm(yacc[:, t, :], ["yacc%d" % t], Rr, aff="dve"):
                pass
            if t < NTP:
                finals.append(DMA("sp", y_p[t * 128:(t + 1) * 128, :], yacc[:, t, :], reads=["yacc%d" % t]))
            else:
                for b in range(2):
                    finals.append(DMA("sp", y_s[b], yacc[b * 32:b * 32 + 16, t, :], reads=["yacc%d" % t]))

        nhr = [0]
        pending = []

        def mlp_pass(grp, bidx, final_block):
            bi = grp % 2
            c0, ncol, tiles = blocks[bidx]
            if grp == 0 and bidx == 0:
                build_x1T(tiles)
            for fc in range(8):
                bank = 1 + (fc % 2)
                for kc in range(8):
                    I("pe", "matmul", PB[bank][:, 0:ncol], lhsT=wup(bi, kc)[:, fc * 128:(fc + 1) * 128], rhs=x1T[:, kc, c0:c0 + ncol],
                      start=(kc == 0), stop=(kc == 7), reads=["wup%d_%d" % (bi, kc)] + ["x1T%d" % tt for tt in tiles],
                      writes=[RB[bank]])
                hrb = hr[nhr[0] % 2]
                hrn = "hr%d" % (nhr[0] % 2)
                nhr[0] += 1
                I("act", "activation", hrb[:, 0:ncol], PB[bank][:, 0:ncol], AF.Relu, reads=[RB[bank]], writes=[hrn])
                I("pool", "tensor_tensor", hT[:, fc, 0:ncol], hrb[:, 0:ncol], hrb[:, 0:ncol], op=ALU.mult,
                  reads=[hrn], writes=["hT%d" % fc])
                if pending and fc % 2 == 1:
                    epilogue(pending.pop(0))
            if grp == 0 and bidx + 1 < len(blocks):
                build_x1T(blocks[bidx + 1][2])
            if grp == 0 and bidx == 0:
                load_mlp_weights(1, extra_w=WO_ALL)
            for ti, t in enumerate(tiles):
                Rr = 64 if t == NTP else 128
                for hf in range(2):
                    bank = 3 + 2 * (ti % 2) + hf
                    for fc in range(8):
                        I("pe", "matmul", PB[bank][0:Rr, :], lhsT=hT[:, fc, ti * 128:ti * 128 + Rr],
                          rhs=wdn(bi, fc)[:, hf * 512:(hf + 1) * 512], start=(fc == 0), stop=(fc == 7),
                          reads=["hT%d" % fc, "wdn%d_%d" % (bi, fc)], writes=[RB[bank]])
                    ysl = yacc[0:Rr, t, hf * 512:(hf + 1) * 512]
                    if grp == 0:
                        I("dve", "scalar_tensor_tensor", ysl, ysl, ALPHA, PB[bank][0:Rr, :], op0=ALU.mult, op1=ALU.add,
                          reads=[RB[bank], "yacc%d" % t], writes=["yacc%d" % t])
                    else:
                        I("dve", "tensor_tensor", ysl, ysl, PB[bank][0:Rr, :], op=ALU.add,
                          reads=[RB[bank], "yacc%d" % t], writes=["yacc%d" % t])
                if grp == 3 and final_block:
                    while pending:
                        epilogue(pending.pop(0))
                    pending.append(t)
            if grp == 3 and not final_block:
                pending.extend(tiles)

        nb = len(blocks)
        for bidx in range(nb):
            mlp_pass(0, bidx, False)
        load_mlp_weights(2)
        for bidx in range(nb):
            mlp_pass(1, bidx, False)
        load_mlp_weights(3)
        lead = min(3, nb)
        for bidx in range(lead):
            mlp_pass(2, bidx, False)
        for bidx in range(lead):
            mlp_pass(3, bidx, bidx == nb - 1)
        for bidx in range(lead, nb):
            mlp_pass(2, bidx, False)
            mlp_pass(3, bidx, bidx == nb - 1)
        while pending:
            epilogue(pending.pop(0))

        P.emit(nc, finals)
        print("ops per engine:", P.stats)
    return nc


_NC_CACHE = {}


def kernel(x_prompt, x_sample, cache_swa_k, cache_swa_v, state_ret, w_in, ret_gn_w, swa_sinks,
           w_out, ln1_w, ln1_b, w_up, w_down, ln2_w, ln2_b):
    f = lambda a: np.ascontiguousarray(np.asarray(a, dtype=np.float32))
    x_prompt, x_sample = f(x_prompt), f(x_sample)
    cache_swa_k, cache_swa_v, state_ret = f(cache_swa_k), f(cache_swa_v), f(state_ret)
    if "nc" not in _NC_CACHE:
        _NC_CACHE["nc"] = build_nc()
    nc = _NC_CACHE["nc"]
    ctab, rtab = make_tables()
    ident = np.eye(128, dtype=np.float32)
    shared = {
        "w_in": f(w_in[0]), "w_out": f(w_out[0]), "w_up": f(w_up[0]), "w_down": f(w_down[0]),
        "gnw": f(ret_gn_w[0]).reshape(1, 512), "sinks": f(swa_sinks[0]).reshape(1, 8),
        "ln1w": f(ln1_w[0]).reshape(1, D), "ln1b": f(ln1_b[0]).reshape(1, D),
        "ln2w": f(ln2_w[0]).reshape(1, D), "ln2b": f(ln2_b[0]).reshape(1, D),
        "ctab": ctab, "rtab": rtab, "ident": ident,
    }
    in_maps = []
    for c in range(8):
        m = dict(shared)
        m["xp"] = x_prompt[c]
        m["xs"] = x_sample[2 * c:2 * c + 2]
        m["ck"] = cache_swa_k[0, 2 * c:2 * c + 2].reshape(2, 128, 128)
        m["cv"] = cache_swa_v[0, 2 * c:2 * c + 2].reshape(2, 128, 128)
        m["st"] = state_ret[0, 2 * c:2 * c + 2]
        in_maps.append(m)
    res = run_bass_kernel_spmd(nc, in_maps, core_ids=list(range(8)))
    rs = res.results
    y_p = np.stack([rs[c]["y_p"] for c in range(8)], axis=0)
    y_s = np.concatenate([rs[c]["y_s"] for c in range(8)], axis=0)
    k_p = np.stack([rs[c]["k_p"].reshape(128, 2, 64) for c in range(8)], axis=0)[None]
    v_p = np.stack([rs[c]["v_p"].reshape(128, 2, 64) for c in range(8)], axis=0)[None]
    r_p = np.stack([rs[c]["ret_p"] for c in range(8)], axis=0)[None]
    k_s = np.concatenate([rs[c]["k_s"].reshape(2, 16, 2, 64) for c in range(8)], axis=0)[None]
    v_s = np.concatenate([rs[c]["v_s"].reshape(2, 16, 2, 64) for c in range(8)], axis=0)[None]
    r_s = np.concatenate([rs[c]["ret_s"] for c in range(8)], axis=0)[None]
    return tuple(np.ascontiguousarray(a, dtype=np.float32) for a in (y_p, y_s, k_p, v_p, r_p, k_s, v_s, r_s))
```

```python
import contextlib
import numpy as np
import concourse.bass as bass
import concourse.mybir as mybir
from concourse.bass_utils import run_bass_kernel_spmd

F32 = mybir.dt.float32
BF16 = mybir.dt.bfloat16
AF = mybir.ActivationFunctionType
ALU = mybir.AluOpType

ENGS = ("pe", "act", "dve", "pool", "sp")

D = 1024
SEQ = 2048
NTP = 16
NT = 17
PROJ_W = 2816
D_FF = 4096
ALPHA = 2.0 ** 0.25
LN_EPS = 1e-5
GN_EPS = 1e-5
PAST = 2048


class Op:
    __slots__ = ("eng", "meth", "args", "kw", "deps", "is_dma", "needs_inc", "sem", "val")

    def __init__(self, eng, meth, args, kw, is_dma):
        self.eng = eng
        self.meth = meth
        self.args = args
        self.kw = kw
        self.deps = []
        self.is_dma = is_dma
        self.needs_inc = is_dma
        self.sem = None
        self.val = None


class Res:
    __slots__ = ("name", "last_w", "readers")

    def __init__(self, name):
        self.name = name
        self.last_w = None
        self.readers = []


class Prog:
    def __init__(self, n_dma_sems=64):
        self.ops = []
        self.n_dma_sems = n_dma_sems
        self.res = {}
        self.barrier_op = None

    def R(self, name):
        r = self.res.get(name)
        if r is None:
            r = Res(name)
            r.last_w = self.barrier_op
            self.res[name] = r
        return r

    def op(self, eng, meth, args, kw, reads=(), writes=(), dma=False):
        o = Op(eng, meth, args, kw, dma)
        reads = [self.R(r) if isinstance(r, str) else r for r in reads]
        writes = [self.R(w) if isinstance(w, str) else w for w in writes]
        deps = []
        for r in reads:
            if r.last_w is not None:
                deps.append(r.last_w)
            if r.name.startswith("pb"):
                deps.extend(x for x in r.readers if x.eng != eng)
        for w in writes:
            if w.last_w is not None:
                deps.append(w.last_w)
            deps.extend(w.readers)
        for r in reads:
            r.readers.append(o)
        for w in writes:
            w.last_w = o
            w.readers = []
        seen = set()
        for d in deps:
            if d is o or id(d) in seen:
                continue
            if (not d.is_dma) and (not dma) and d.eng == "pe" and eng == "pe":
                continue
            seen.add(id(d))
            o.deps.append(d)
            d.needs_inc = True
        self.ops.append(o)
        return o

    def barrier(self, skip=()):
        o = self.op("sp", "nop", (), {"nofuse": True}, reads=(),
                    writes=[r for r in self.res.values() if not r.name.startswith(tuple(skip))])
        self.barrier_op = o
        o.needs_inc = True
        return o

    def emit(self, nc, finals):
        import os as _os
        _tr = int(_os.environ.get("KTRUNC", "0"))
        if _tr:
            keep = set(id(o) for o in self.ops[:_tr])
            self.ops = self.ops[:_tr]
            finals = [f for f in finals if id(f) in keep]
            print("TRUNCATED to", _tr, "ops; last:", self.ops[-1].eng, self.ops[-1].meth)
        ops = self.ops
        st = contextlib.ExitStack()
        eng_sem = {e: st.enter_context(nc.semaphore("s_" + e)) for e in ENGS}
        dma_sems = [st.enter_context(nc.semaphore("d%d" % i)) for i in range(self.n_dma_sems)]
        cnt = {e: 0 for e in ENGS}
        dma_use = [0] * self.n_dma_sems
        dma_prev = [None] * self.n_dma_sems
        half = self.n_dma_sems // 2
        kq = {"pool": 0, "hw": 0}
        for o in ops:
            if o.is_dma:
                if o.eng == "pool":
                    s = kq["pool"] % half
                    kq["pool"] += 1
                else:
                    s = half + kq["hw"] % (self.n_dma_sems - half)
                    kq["hw"] += 1
                dma_use[s] += 1
                o.sem = ("d", s)
                o.val = 16 * dma_use[s]
                if dma_prev[s] is not None:
                    o.deps.append(dma_prev[s])
                dma_prev[s] = o
            elif o.needs_inc:
                cnt[o.eng] += 1
                o.sem = ("e", o.eng)
                o.val = cnt[o.eng]

        def semh(key):
            return eng_sem[key[1]] if key[0] == "e" else dma_sems[key[1]]

        vc_of = {}
        eng_known = {e: {} for e in ENGS}
        plans = {e: [] for e in ENGS}
        for o in ops:
            kn = eng_known[o.eng]
            wm = {}
            for d in o.deps:
                key, val = d.sem, d.val
                if kn.get(key, 0) >= val:
                    continue
                if wm.get(key, 0) < val:
                    wm[key] = val
                for k2, v2 in vc_of[id(d)].items():
                    if kn.get(k2, 0) < v2:
                        kn[k2] = v2
            plans[o.eng].append((list(wm.items()), o))
            if o.sem is not None:
                v = dict(kn)
                if v.get(o.sem, 0) < o.val:
                    v[o.sem] = o.val
                vc_of[id(o)] = v
        self.stats = {e: len(plans[e]) for e in ENGS}
        self.stats["waits"] = sum(len(w) for e in ENGS for w, _ in plans[e])

        with st, nc.Block() as block:
            def mk(ename):
                def body(eng):
                    for waits, o in plans[ename]:
                        for key, val in waits:
                            eng.wait_ge(semh(key), val)
                        ins = getattr(eng, o.meth)(*o.args, **o.kw)
                        if o.sem is not None:
                            ins.then_inc(semh(o.sem), 16 if o.is_dma else 1)
                    if ename == "sp":
                        for f in finals:
                            eng.wait_ge(semh(f.sem), f.val)
                return body
            block.tensor(mk("pe"))
            block.scalar(mk("act"))
            block.vector(mk("dve"))
            block.gpsimd(mk("pool"))
            block.sync(mk("sp"))


CT = {}
_off = 0
for _n, _w in [("cm", 128), ("cm_s", 64), ("nb_s", 2), ("nhalf", 4)]:
    CT[_n] = (_off, _w)
    _off += _w
NCT = _off
RTW = 1040


def _lg():
    return np.log1p(-(2.0 ** (-5.0 - np.arange(4, dtype=np.float64))))


def make_tables():
    lg = _lg()
    tab = np.zeros((128, NCT), np.float64)

    def put(name, arr):
        o, w = CT[name]
        arr = np.asarray(arr, np.float64)
        tab[:arr.shape[0], o:o + w] = arr.reshape(arr.shape[0], w)

    p = np.arange(128)
    inv_r = (10000.0 ** (-np.arange(64, dtype=np.float32) / 64)).astype(np.float32)
    inv_s = (500000.0 ** (-np.arange(8, dtype=np.float32) / 8)).astype(np.float32)
    rt = np.zeros((NT, 128, RTW), np.float64)
    pos = (np.arange(NTP)[:, None] * 128 + p[None, :]).astype(np.float32)
    ang_r = (pos[:, :, None] * inv_r[None, None, :]).astype(np.float32).astype(np.float64)
    ang_s = (pos[:, :, None] * inv_s[None, None, :]).astype(np.float32).astype(np.float64)
    qdec = np.exp(lg[None, :] * (p[:, None] - 127.0))
    kdec = (128.0 ** -0.5) * np.exp(lg[None, :] * (127.0 - p[:, None]))
    cr, sr = np.cos(ang_r), np.sin(ang_r)
    rt[:NTP, :, 0:256] = (cr[:, :, None, :] * qdec[None, :, :, None]).reshape(NTP, 128, 256)
    rt[:NTP, :, 256:512] = (sr[:, :, None, :] * qdec[None, :, :, None]).reshape(NTP, 128, 256)
    rt[:NTP, :, 512:768] = (cr[:, :, None, :] * kdec[None, :, :, None]).reshape(NTP, 128, 256)
    rt[:NTP, :, 768:1024] = (sr[:, :, None, :] * kdec[None, :, :, None]).reshape(NTP, 128, 256)
    rt[:NTP, :, 1024:1032] = np.cos(ang_s)
    rt[:NTP, :, 1032:1040] = np.sin(ang_s)
    rows = np.arange(64)
    i_loc = rows % 32
    valid = i_loc < 16
    b_of = rows // 32
    pos_s = (PAST + np.minimum(i_loc, 15)).astype(np.float32)
    ang_rs = (pos_s[:, None] * inv_r[None, :]).astype(np.float32).astype(np.float64)
    ang_ss = (pos_s[:, None] * inv_s[None, :]).astype(np.float32).astype(np.float64)
    qdec_s = np.where(valid[:, None], np.exp(lg[None, :] * (i_loc[:, None] - 15.0)), 0.0)
    kdec_s = np.where(valid[:, None], (128.0 ** -0.5) * np.exp(lg[None, :] * (15.0 - i_loc[:, None])), 0.0)
    crs, srs = np.cos(ang_rs), np.sin(ang_rs)
    rt[NTP, :64, 0:256] = (crs[:, None, :] * qdec_s[:, :, None]).reshape(64, 256)
    rt[NTP, :64, 256:512] = (srs[:, None, :] * qdec_s[:, :, None]).reshape(64, 256)
    rt[NTP, :64, 512:768] = (crs[:, None, :] * kdec_s[:, :, None]).reshape(64, 256)
    rt[NTP, :64, 768:1024] = (srs[:, None, :] * kdec_s[:, :, None]).reshape(64, 256)
    rt[NTP, :64, 1024:1032] = np.cos(ang_ss)
    rt[NTP, :64, 1032:1040] = np.sin(ang_ss)
    put("cm", (p[None, :] >= p[:, None]).astype(np.float64))
    cms = (valid[:, None] & valid[None, :] & (b_of[:, None] == b_of[None, :]) & (i_loc[None, :] >= i_loc[:, None]))
    put("cm_s", cms.astype(np.float64))
    put("nb_s", np.where(valid[:, None] & (b_of[:, None] == np.arange(2)[None, :]), 0.0, -30000.0))
    o, w = CT["nhalf"]
    tab[:, o:o + w] = -0.5
    return tab.astype(np.float32), rt.astype(np.float32)


GC128 = [float(np.exp(_lg()[h] * 128.0)) for h in range(4)]
GC16 = [float(np.exp(_lg()[h] * 16.0)) for h in range(4)]
COLG = [(0, 512), (512, 1024), (1024, 1536), (1536, 2048), (2048, 2560), (2560, 2816)]


def build_nc():
    nc = bass.Bass("TRN2", target_bir_lowering=False)

    def din(name, shape):
        return nc.dram_tensor(name, list(shape), F32, kind="ExternalInput").ap()

    def dout(name, shape):
        return nc.dram_tensor(name, list(shape), F32, kind="ExternalOutput").ap()

    xp = din("xp", [SEQ, D])
    xs = din("xs", [2, 16, D])
    ck = din("ck", [2, 128, 128])
    cv = din("cv", [2, 128, 128])
    st_in = din("st", [2, 4, 128, 128])
    w_in = din("w_in", [D, PROJ_W])
    w_out = din("w_out", [D, D])
    w_up = din("w_up", [D, D_FF])
    w_down = din("w_down", [D_FF, D])
    gnw = din("gnw", [1, 512])
    sinks = din("sinks", [1, 8])
    ln1w = din("ln1w", [1, D])
    ln1b = din("ln1b", [1, D])
    ln2w = din("ln2w", [1, D])
    ln2b = din("ln2b", [1, D])
    ctab_d = din("ctab", [128, NCT])
    rtab_d = din("rtab", [NT, 128, RTW])
    ident_d = din("ident", [128, 128])

    y_p = dout("y_p", [SEQ, D])
    y_s = dout("y_s", [2, 16, D])
    k_p = dout("k_p", [128, 128])
    v_p = dout("v_p", [128, 128])
    ret_p = dout("ret_p", [4, 128, 128])
    k_s = dout("k_s", [2, 16, 128])
    v_s = dout("v_s", [2, 16, 128])
    ret_s = dout("ret_s", [2, 4, 128, 128])

    P = Prog()
    finals = []

    def I(eng, meth, *args, reads=(), writes=(), dma=False, **kw):
        return P.op(eng, meth, args, kw, reads=reads, writes=writes, dma=dma)

    def DMA(eng, out, in_, reads=(), writes=()):
        return P.op(eng, "dma_start", (), {"out": out, "in_": in_}, reads=reads, writes=writes, dma=True)

    es = contextlib.ExitStack()
    with es:
        def sb(name, shape, dt, stack=None):
            return (stack or es).enter_context(nc.sbuf_tensor(name, list(shape), dt))

        WA = sb("WA", [128, 8 * PROJ_W], BF16)
        Win = WA[:, :].rearrange("p (k e) -> p k e", k=8)
        WB = sb("WB", [128, 8192], BF16)
        Wo = WB[:, :].rearrange("p (k e) -> p k e", k=8)
        yacc = sb("yacc", [128, NT, D], F32)
        ctab = sb("ctab_sb", [128, NCT], F32)
        ident = sb("ident_sb", [128, 128], BF16)
        identf = sb("identf_sb", [128, 128], F32)
        lnw_b = sb("lnw_b", [128, D], F32)
        lnb_b = sb("lnb_b", [128, D], F32)
        lst = sb("lst", [128, 2, 6], F32)
        lmv = sb("lmv", [128, 2], F32)
        lve = sb("lve", [128, 1], F32)
        lrs = sb("lrs", [128, 1], F32)
        lnm = sb("lnm", [128, 1], F32)
        epsc = sb("epsc", [128, 2], F32)
        epsg, epsl = epsc[:, 0:1], epsc[:, 1:2]
        lnd = sb("lnd", [1, 2], F32)
        gsc = sb("gsc", [1, 2], F32)
        WIN_ALL = ["win_%d_%d" % (kc, c0) for (c0, c1) in COLG for kc in range(8)]
        WO_ALL = ["wo_%d" % kc for kc in range(8)]

        def C(name, rows=128):
            o, w = CT[name]
            return ctab[0:rows, o:o + w]

        PB = [es.enter_context(nc.psum_tensor("pb%d" % i, [128, 512], F32)) for i in range(8)]
        PBb = [PB[i][:, :].bitcast(BF16) for i in range(8)]
        RB = ["pb%d" % i for i in range(8)]

        es1 = contextlib.ExitStack()

        def sb1(name, shape, dt):
            return sb(name, shape, dt, es1)

        zo = sb1("zo", [1, 128], BF16)
        gnw_b = sb1("gnw_b", [128, 512], F32)
        sink_f = sb1("sink_f", [1, 8], F32)
        esink16 = sb1("esink16", [1, 128], BF16)
        rtab = [sb1("rtab_sb%d" % i, [128, RTW], F32) for i in range(2)]
        xf = [sb1("xf%d" % i, [128, D], F32) for i in range(2)]
        xT = sb1("xT", [128, 8, 128], BF16)
        rt = [sb1("rt%d" % i, [128, 4, 64], F32) for i in range(4)]
        RT = ["rt0", "rt1", "rt2", "rt3"]
        q_tm = sb1("q_tm", [128, 512], BF16)
        k_tm = [sb1("k_tm%d" % i, [128, 512], BF16) for i in range(2)]
        v_tm = [sb1("v_tm%d" % i, [128, 512], BF16) for i in range(2)]
        gw = [sb1("gw%d" % i, [128, 512], F32) for i in range(2)]
        sq_tm = sb1("sq_tm", [128, 512], BF16)
        skf = sb1("skf", [128, 128], F32)
        svf = sb1("svf", [128, 128], F32)
        sk_tm = sb1("sk_tm", [128, 128], BF16)
        srt = [rt[i][:, :, :].rearrange("p h i -> p (h i)")[:, 0:64].rearrange("p (h i) -> p h i", i=8) for i in range(4)]
        SRT = RT
        qT = [sb1("qT%d" % i, [128, 4, 128], BF16) for i in range(2)]
        kT = [sb1("kT%d" % i, [128, 4, 128], BF16) for i in range(2)]
        sqT = [sb1("sqT%d" % i, [64, 8, 128], BF16) for i in range(2)]
        skT = [sb1("skT%d" % i, [64, 2, 128], BF16) for i in range(3)]
        svcx = [sb1("svcx%d" % i, [64, 2, 128], BF16) for i in range(6)]
        scT = sb1("scT", [128, 4, 128], BF16)
        S = sb1("S", [128, 4, 128], F32)
        Sg = sb1("Sg", [128, 4, 128], BF16)
        gst = sb1("gst", [128, 4, 6], F32)
        gmv = sb1("gmv", [128, 4, 2], F32)
        gve = sb1("gve", [128, 4], F32)
        grs = sb1("grs", [128, 4], F32)
        on = sb1("on", [128, 512], F32)
        mix_tm = sb1("mix_tm", [128, 512], BF16)
        mixT2 = [sb1("mixT%d" % i, [128, 4, 128], BF16) for i in range(2)]
        PT = sb1("PT", [64, 3, 256], BF16)
        PT2 = sb1("PT2", [64, 3, 256], BF16)
        rden = sb1("rden", [128, 512], F32)
        esink = rden[0:1, 0:256].bitcast(BF16)
        swaT2 = [sb1("swaT%d" % i, [128, 4, 128], BF16) for i in range(2)]
        S_s1 = sb1("S_s1", [128, 4, 128], F32)
        S_s = [S, S_s1]
        Sg_s1 = sb1("Sg_s1", [128, 4, 128], BF16)
        Sg_s = [Sg, Sg_s1]
        qTm = [sb1("qTm%d" % b, [128, 4, 64], BF16) for b in range(2)]
        ckb = sb1("ckb", [128, 128], BF16)
        cvx = [sb1("cvx%d" % b, [128, 2, 128], BF16) for b in range(2)]
        SVS = (2 * NTP) % 6
        svx_s = svcx[SVS]
        ckT = [PT[:, b, :].rearrange("p (g n) -> p g n", g=2) for b in range(2)]
        PTn = PT[:, 2, 0:64]
        PTc = scT[:, 0, 0:64]
        print("sbuf remaining after phase-1 alloc:", nc.sbuf_bytes_remaining)

        DMA("sp", ctab[:, :], ctab_d, writes=["ctab"])
        DMA("pool", ident[:, :], ident_d, writes=["ident"])
        DMA("sp", identf[:, :], ident_d, writes=["identf"])
        DMA("sp", xf[0][:, :], xp[0:128, :], writes=["xf0"])
        DMA("sp", rtab[0][:, :], rtab_d[0], writes=["rtab0"])
        for (c0, c1) in COLG:
            for kc in range(8):
                DMA("pool", Win[:, kc, c0:c1], w_in[kc * 128:(kc + 1) * 128, c0:c1], writes=["win_%d_%d" % (kc, c0)])
        DMA("sp", gnw_b[:, :], gnw.broadcast_to([128, 512]), writes=["gnw_b"])
        DMA("sp", lnw_b[:, :], ln1w.broadcast_to([128, D]), writes=["lnw_b"])
        DMA("sp", lnb_b[:, :], ln1b.broadcast_to([128, D]), writes=["lnb_b"])
        DMA("sp", sink_f[:, :], sinks, writes=["sink_f"])
        for kc in range(8):
            DMA("pool", Wo[:, kc, :], w_out[kc * 128:(kc + 1) * 128, :], writes=["wo_%d" % kc])
        I("act", "activation", sink_f[:, :], sink_f[:, :], AF.Exp, reads=["sink_f"], writes=["sink_f"])
        I("dve", "tensor_copy", esink[0:1, :].rearrange("p (h q) -> p h q", q=64),
          sink_f[0:1, :].unsqueeze(2).broadcast_to([1, 8, 64]), reads=["sink_f"], writes=["esink"])
        I("dve", "tensor_copy", esink16[0:1, :].rearrange("p (h q) -> p h q", q=16),
          sink_f[0:1, :].unsqueeze(2).broadcast_to([1, 8, 16]), reads=["sink_f"], writes=["esink16"])
        I("pool", "memset", epsc[:, 0:1], GN_EPS, writes=["epsc"])
        I("pool", "memset", epsc[:, 1:2], LN_EPS, writes=["epsc"])
        I("pool", "memset", lnd[:, :], 1.0, writes=["lnd"])
        I("pool", "memset", zo[0:1, 0:64], 0.0, writes=["zo"])
        I("pool", "memset", zo[0:1, 64:128], 1.0, writes=["zo"])
        I("pool", "memset", S[:, :, :], 0.0, writes=["S"])
        for i in range(2):
            I("pool", "memset", swaT2[i][:, :, :], 0.0, writes=["swaT%d" % i])
        for i in range(6):
            I("pool", "memset", svcx[i][:, :, :], 1.0, writes=["svcx%d" % i])
        for b in range(2):
            I("pool", "memset", cvx[b][:, :, :], 1.0, writes=["cvx%d" % b])

        def front(t):
            smp = (t == NTP)
            Rr = 64 if smp else 128
            p = t % 2
            rtb, rtn = rtab[p], "rtab%d" % p
            idR = ident[0:Rr, 0:Rr]
            xfp, xfn = xf[p], "xf%d" % p
            for rnd in range(2):
                for k4 in range(4):
                    kc = rnd * 4 + k4
                    I("pe", "transpose", PB[0][:, k4 * 128:k4 * 128 + Rr], xfp[0:Rr, kc * 128:(kc + 1) * 128], identf[0:Rr, 0:Rr],
                      reads=[xfn, "identf"], writes=[RB[0]])
                I("act", "copy", xT[:, rnd * 4:(rnd + 1) * 4, 0:Rr], PB[0][:, :].rearrange("p (k n) -> p k n", k=4)[:, :, 0:Rr],
                  reads=[RB[0]], writes=["xT"])
            I("act", "mul", yacc[0:Rr, t, :], xfp[0:Rr, :], ALPHA, reads=[xfn], writes=["yacc%d" % t])
            tn = t + 1
            if tn < NTP:
                DMA("sp", xf[tn % 2][:, :], xp[tn * 128:(tn + 1) * 128, :], writes=["xf%d" % (tn % 2)])
            elif tn == NTP:
                I("pool", "memset", xf[tn % 2][0:64, :], 0.0, writes=["xf%d" % (tn % 2)])
                for b in range(2):
                    DMA("sp", xf[tn % 2][b * 32:b * 32 + 16, :], xs[b], writes=["xf%d" % (tn % 2)])
            if tn < NT:
                DMA("sp", rtab[tn % 2][:, :], rtab_d[tn], writes=["rtab%d" % (tn % 2)])
            yield

            def proj(gi, bank):
                c0, c1 = COLG[gi]
                for kc in range(8):
                    I("pe", "matmul", PB[bank][0:Rr, 0:c1 - c0], lhsT=xT[:, kc, 0:Rr], rhs=Win[:, kc, c0:c1],
                      start=(kc == 0), stop=(kc == 7), reads=["xT", "win_%d_%d" % (kc, c0)], writes=[RB[bank]])

            def rope_ret(bank, toff, dst, dname):
                ps4 = PB[bank][0:Rr, :].rearrange("p (h two i) -> p h two i", h=4, two=2)
                dst4 = dst[0:Rr, :].rearrange("p (h two i) -> p h two i", h=4, two=2)
                cb = rtb[0:Rr, toff:toff + 256].rearrange("p (h i) -> p h i", h=4)
                sbb = rtb[0:Rr, toff + 256:toff + 512].rearrange("p (h i) -> p h i", h=4)
                x1, x2 = ps4[:, :, 0, :], ps4[:, :, 1, :]
                I("dve", "tensor_tensor", rt[0][0:Rr], x1, cb, op=ALU.mult, reads=[RB[bank], rtn], writes=[RT[0]])
                I("dve", "tensor_tensor", rt[1][0:Rr], x2, sbb, op=ALU.mult, reads=[RB[bank], rtn], writes=[RT[1]])
                I("dve", "tensor_tensor", rt[2][0:Rr], x2, cb, op=ALU.mult, reads=[RB[bank], rtn], writes=[RT[2]])
                I("dve", "tensor_tensor", rt[3][0:Rr], x1, sbb, op=ALU.mult, reads=[RB[bank], rtn], writes=[RT[3]])
                I("dve", "tensor_tensor", dst4[:, :, 0, :], rt[0][0:Rr], rt[1][0:Rr], op=ALU.subtract,
                  reads=[RT[0], RT[1]], writes=[dname])
                I("dve", "tensor_tensor", dst4[:, :, 1, :], rt[2][0:Rr], rt[3][0:Rr], op=ALU.add,
                  reads=[RT[2], RT[3]], writes=[dname])

            def rope_swa(ps_view, dst_view, nh, bankres, dname):
                cb = rtb[0:Rr, 1024:1032].unsqueeze(1).broadcast_to([Rr, nh, 8])
                sbb = rtb[0:Rr, 1032:1040].unsqueeze(1).broadcast_to([Rr, nh, 8])
                x1, x2 = ps_view[:, :, 0:8], ps_view[:, :, 8:16]
                tt = [srt[i][0:Rr, 0:nh, :] for i in range(4)]
                I("dve", "tensor_tensor", tt[0], x1, cb, op=ALU.mult, reads=[bankres, rtn], writes=[SRT[0]])
                I("dve", "tensor_tensor", tt[1], x2, sbb, op=ALU.mult, reads=[bankres, rtn], writes=[SRT[1]])
                I("dve", "tensor_tensor", tt[2], x2, cb, op=ALU.mult, reads=[bankres, rtn], writes=[SRT[2]])
                I("dve", "tensor_tensor", tt[3], x1, sbb, op=ALU.mult, reads=[bankres, rtn], writes=[SRT[3]])
                I("dve", "tensor_tensor", dst_view[:, :, 0:8], tt[0], tt[1], op=ALU.subtract, reads=[SRT[0], SRT[1]], writes=[dname])
                I("dve", "tensor_tensor", dst_view[:, :, 8:16], tt[2], tt[3], op=ALU.add, reads=[SRT[2], SRT[3]], writes=[dname])

            proj(0, 1)
            rope_ret(1, 0, q_tm, "q_tm")
            yield
            proj(1, 2)
            for h in range(4):
                I("pe", "transpose", PBb[0][:, h * 128:h * 128 + Rr], q_tm[0:Rr, h * 128:(h + 1) * 128], idR,
                  reads=["q_tm", "ident"], writes=[RB[0]])
            rope_ret(2, 512, k_tm[p], "k_tm%d" % p)
            yield
            proj(2, 1)
            for h in range(4):
                I("pe", "transpose", PBb[0][:, 512 + h * 128:512 + h * 128 + Rr], k_tm[p][0:Rr, h * 128:(h + 1) * 128], idR,
                  reads=["k_tm%d" % p, "ident"], writes=[RB[0]])
            I("act", "copy", v_tm[p][0:Rr, :], PB[1][0:Rr, :], reads=[RB[1]], writes=["v_tm%d" % p])
            I("act", "copy", qT[p][:, :, 0:Rr], PBb[0][:, 0:512].rearrange("p (h n) -> p h n", h=4)[:, :, 0:Rr],
              reads=[RB[0]], writes=["qT%d" % p])
            I("act", "copy", kT[p][:, :, 0:Rr], PBb[0][:, 512:1024].rearrange("p (h n) -> p h n", h=4)[:, :, 0:Rr],
              reads=[RB[0]], writes=["kT%d" % p])
            yield
            prev = t - 2
            emb = (prev >= 0) and (prev <= NTP - 2)
            if emb:
                out_proj_half(prev, 0, 0, mm=True, add=False)
            proj(3, 2)
            I("act", "activation", gw[p][0:Rr, :], PB[2][0:Rr, :], AF.Silu, reads=[RB[2]], writes=["gw%d" % p])
            I("act", "activation", lnd[:, 1:2], lnd[:, 0:1], AF.Ln, reads=["lnd"], writes=["lnd2"])
            I("pool", "tensor_tensor", gw[p][0:Rr, :], gw[p][0:Rr, :], gnw_b[0:Rr, :], op=ALU.mult,
              reads=["gw%d" % p, "gnw_b"], writes=["gw%d" % p])
            yield
            if emb:
                out_proj_half(prev, 0, 0, mm=False, add=True)
                out_proj_half(prev, 1, 0, mm=True, add=False)
            proj(4, 1)
            I("act", "copy", sq_tm[0:Rr, :], PB[1][0:Rr, :], reads=[RB[1]], writes=["sq_tm"])
            rope_swa(PB[1][0:Rr, :].rearrange("p (h d) -> p h d", d=64), sq_tm[0:Rr, :].rearrange("p (h d) -> p h d", d=64),
                     8, RB[1], "sq_tm")
            yield
            if emb:
                out_proj_half(prev, 1, 0, mm=False, add=True)
            proj(5, 2)
            need_f32 = smp or (t == NTP - 1)
            I("act", "copy", sk_tm[0:Rr, :], PB[2][0:Rr, 0:128], reads=[RB[2]], writes=["sk_tm"])
            rope_swa(PB[2][0:Rr, 0:128].rearrange("p (h d) -> p h d", d=64), sk_tm[0:Rr, :].rearrange("p (h d) -> p h d", d=64),
                     2, RB[2], "sk_tm")
            if not smp:
                for cc in range(2):
                    c = 2 * t + cc
                    I("act", "copy", svcx[c % 6][:, :, 0:64],
                      PB[2][cc * 64:(cc + 1) * 64, 128:256].rearrange("p (g d) -> p g d", g=2),
                      reads=[RB[2]], writes=["svcx%d" % (c % 6)])
            if need_f32:
                I("act", "copy", skf[0:Rr, :], PB[2][0:Rr, 0:128], reads=[RB[2]], writes=["skf"])
                I("act", "copy", svf[0:Rr, :], PB[2][0:Rr, 128:256], reads=[RB[2]], writes=["svf"])
                skv = skf[0:Rr, :].rearrange("p (h d) -> p h d", d=64)
                I("dve", "tensor_tensor", skv[:, :, 0:8], srt[0][0:Rr, 0:2, :], srt[1][0:Rr, 0:2, :], op=ALU.subtract,
                  reads=[SRT[0], SRT[1]], writes=["skf"])
                I("dve", "tensor_tensor", skv[:, :, 8:16], srt[2][0:Rr, 0:2, :], srt[3][0:Rr, 0:2, :], op=ALU.add,
                  reads=[SRT[2], SRT[3]], writes=["skf"])
                if smp:
                    I("pool", "tensor_copy", svx_s[:, :, 0:64], svf[0:64, :].rearrange("p (g d) -> p g d", g=2),
                      reads=["svf"], writes=["svcx%d" % SVS])
                    for b in range(2):
                        finals.append(DMA("sp", k_s[b], skf[b * 32:b * 32 + 16, :], reads=["skf"]))
                        finals.append(DMA("sp", v_s[b], svf[b * 32:b * 32 + 16, :], reads=["svf"]))
                else:
                    finals.append(DMA("sp", k_p, skf[:, :], reads=["skf"]))
                    finals.append(DMA("sp", v_p, svf[:, :], reads=["svf"]))
            yield
            for h in range(8):
                I("pe", "transpose", PBb[0][0:64, h * 128:h * 128 + Rr], sq_tm[0:Rr, h * 64:(h + 1) * 64], idR,
                  reads=["sq_tm", "ident"], writes=[RB[0]])
            for g in range(2):
                I("pe", "transpose", PBb[1][0:64, g * 128:g * 128 + Rr], sk_tm[0:Rr, g * 64:(g + 1) * 64], idR,
                  reads=["sk_tm", "ident"], writes=[RB[1]])
            I("act", "copy", sqT[p][:, :, 0:Rr], PBb[0][0:64, :].rearrange("p (h n) -> p h n", h=8)[:, :, 0:Rr],
              reads=[RB[0]], writes=["sqT%d" % p])
            I("dve", "tensor_copy", skT[t % 3][:, :, 0:Rr], PBb[1][0:64, 0:256].rearrange("p (g n) -> p g n", g=2)[:, :, 0:Rr],
              reads=[RB[1]], writes=["skT%d" % (t % 3)])
            yield

        def groupnorm_and_mix(Rr, p):
            idR = ident[0:Rr, 0:Rr]
            for h in range(4):
                I("dve", "bn_stats", gst[0:Rr, h, :], PB[4][0:Rr, h * 128:(h + 1) * 128], reads=[RB[4]], writes=["gst"])
            for h in range(4):
                I("dve", "bn_aggr", gmv[0:Rr, h, :], gst[0:Rr, h, :], reads=["gst"], writes=["gmv"])
            I("act", "activation", gve[0:Rr, :], gmv[0:Rr, :, 1], AF.Ln, bias=epsg[0:Rr, :], reads=["gmv", "epsc"], writes=["gve"])
            I("act", "activation", grs[0:Rr, :], gve[0:Rr, :], AF.Exp, scale=-0.5, reads=["gve"], writes=["grs"])
            for h in range(4):
                I("dve", "scalar_tensor_tensor", on[0:Rr, h * 128:(h + 1) * 128], PB[4][0:Rr, h * 128:(h + 1) * 128],
                  gmv[0:Rr, h, 0:1], gw[p][0:Rr, h * 128:(h + 1) * 128], op0=ALU.subtract, op1=ALU.mult,
                  reads=[RB[4], "gmv", "gw%d" % p], writes=["on"])
            yield
            I("dve", "tensor_tensor", mix_tm[0:Rr, :].rearrange("p (h e) -> p h e", h=4),
              on[0:Rr, :].rearrange("p (h e) -> p h e", h=4),
              grs[0:Rr, :].unsqueeze(2).broadcast_to([Rr, 4, 128]), op=ALU.mult,
              reads=["on", "grs"], writes=["mix_tm"])
            for h in range(4):
                I("pe", "transpose", PBb[3][:, h * 128:h * 128 + Rr], mix_tm[0:Rr, h * 128:(h + 1) * 128], idR,
                  reads=["mix_tm", "ident"], writes=[RB[3]])
            I("act", "copy", mixT2[p][:, :, 0:Rr], PBb[3][:, 0:512].rearrange("p (h n) -> p h n", h=4)[:, :, 0:Rr],
              reads=[RB[3]], writes=["mixT%d" % p])

        def layer_norm(src, sres, Rr, aff="pool", spread=False):
            for hf in range(2):
                I("dve", "bn_stats", lst[0:Rr, hf, :], src[0:Rr, hf * 512:(hf + 1) * 512], reads=sres, writes=["lst"])
            I("dve", "bn_aggr", lmv[0:Rr, :], lst[0:Rr, :, :], reads=["lst"], writes=["lmv"])
            I("act", "activation", lve[0:Rr, :], lmv[0:Rr, 1:2], AF.Ln, bias=epsl[0:Rr, :], reads=["lmv", "epsc"], writes=["lve"])
            I("act", "activation", lrs[0:Rr, :], lve[0:Rr, :], AF.Exp, scale=-0.5, reads=["lve"], writes=["lrs"])
            yield
            if spread:
                I("act", "activation", lve[0:Rr, :], lmv[0:Rr, 0:1], AF.Identity, scale=lrs[0:Rr, :], reads=["lmv", "lrs"], writes=["lve"])
                I("act", "mul", lnm[0:Rr, :], lve[0:Rr, :], -1.0, reads=["lve"], writes=["lnm"])
                I("act", "activation", src[0:Rr, :], src[0:Rr, :], AF.Identity, bias=lnm[0:Rr, :], scale=lrs[0:Rr, :],
                  reads=sres + ["lnm", "lrs"], writes=sres)
                I("pool", "tensor_tensor", src[0:Rr, :], src[0:Rr, :], lnw_b[0:Rr, :], op=ALU.mult, reads=sres + ["lnw_b"], writes=sres)
                I("pool", "tensor_tensor", src[0:Rr, :], src[0:Rr, :], lnb_b[0:Rr, :], op=ALU.add, reads=sres + ["lnb_b"], writes=sres)
                return
            if aff == "dve":
                I("dve", "scalar_tensor_tensor", src[0:Rr, :], src[0:Rr, :], lmv[0:Rr, 0:1], lnw_b[0:Rr, :],
                  op0=ALU.subtract, op1=ALU.mult, reads=sres + ["lmv", "lnw_b"], writes=sres)
                I("dve", "scalar_tensor_tensor", src[0:Rr, :], src[0:Rr, :], lrs[0:Rr, :], lnb_b[0:Rr, :],
                  op0=ALU.mult, op1=ALU.add, reads=sres + ["lrs", "lnb_b"], writes=sres)
                return
            I("dve", "tensor_scalar", src[0:Rr, :], src[0:Rr, :], lmv[0:Rr, 0:1], lrs[0:Rr, :], op0=ALU.subtract, op1=ALU.mult,
              reads=sres + ["lmv", "lrs"], writes=sres)
            I(aff, "tensor_tensor", src[0:Rr, :], src[0:Rr, :], lnw_b[0:Rr, :], op=ALU.mult, reads=sres + ["lnw_b"], writes=sres)
            I(aff, "tensor_tensor", src[0:Rr, :], src[0:Rr, :], lnb_b[0:Rr, :], op=ALU.add, reads=sres + ["lnb_b"], writes=sres)

        def out_proj_half(t, hf, bank, mm=True, add=True):
            Rr = 64 if t == NTP else 128
            pp = t % 2
            if mm:
                for kc in range(8):
                    lhsT = mixT2[pp][:, kc, 0:Rr] if kc < 4 else swaT2[pp][:, kc - 4, 0:Rr]
                    I("pe", "matmul", PB[bank][0:Rr, :], lhsT=lhsT, rhs=Wo[:, kc, hf * 512:(hf + 1) * 512],
                      start=(kc == 0), stop=(kc == 7), reads=[("mixT%d" if kc < 4 else "swaT%d") % pp, "wo_%d" % kc], writes=[RB[bank]])
            if add:
                ysl = yacc[0:Rr, t, hf * 512:(hf + 1) * 512]
                I("dve", "tensor_tensor", ysl, ysl, PB[bank][0:Rr, :], op=ALU.add, reads=[RB[bank], "yacc%d" % t], writes=["yacc%d" % t])

        def ln1_only(t):
            Rr = 64 if t == NTP else 128
            for _ in layer_norm(yacc[:, t, :], ["yacc%d" % t], Rr, aff="dve"):
                yield

        def out_proj_ln1(t):
            Rr = 64 if t == NTP else 128
            for hf in range(2):
                bank = 1 + hf
                for kc in range(8):
                    lhsT = mixT2[t % 2][:, kc, 0:Rr] if kc < 4 else swaT2[t % 2][:, kc - 4, 0:Rr]
                    I("pe", "matmul", PB[bank][0:Rr, :], lhsT=lhsT, rhs=Wo[:, kc, hf * 512:(hf + 1) * 512],
                      start=(kc == 0), stop=(kc == 7), reads=[("mixT%d" if kc < 4 else "swaT%d") % (t % 2), "wo_%d" % kc], writes=[RB[bank]])
            yield
            for hf in range(2):
                ysl = yacc[0:Rr, t, hf * 512:(hf + 1) * 512]
                I("dve", "tensor_tensor", ysl, ysl, PB[1 + hf][0:Rr, :], op=ALU.add, reads=[RB[1 + hf], "yacc%d" % t], writes=["yacc%d" % t])
            for _ in layer_norm(yacc[:, t, :], ["yacc%d" % t], Rr, aff="dve"):
                yield

        def swa_chain(t, p, g, cc, bufs):
            sc_a, sc_b, sc_b_off, PTb, PTn_ = bufs
            c = 2 * t + cc
            js = [j for j in (c - 2, c - 1, c) if j >= 0]
            nj = len(js)
            rhs_q = sqT[p][:, g * 4:(g + 1) * 4, cc * 64:(cc + 1) * 64]
            slots = [PB[sc_a][0:64, 0:256], PB[sc_a][0:64, 256:512], PB[sc_b][0:64, sc_b_off:sc_b_off + 256]]
            for idx, j in enumerate(js):
                tj, jh = j // 2, j % 2
                I("pe", "matmul", slots[idx], lhsT=skT[tj % 3][:, g, jh * 64:(jh + 1) * 64], rhs=rhs_q,
                  start=True, stop=True, reads=["skT%d" % (tj % 3), "sqT%d" % p], writes=[RB[sc_a] if idx < 2 else RB[sc_b]])
            yield
            n6 = min(nj, 2)
            I("act", "activation", PTb[:, 0:n6, :], PB[sc_a][0:64, 0:n6 * 256].rearrange("p (s n) -> p s n", s=n6),
              AF.Exp, scale=0.125, reads=[RB[sc_a]], writes=[PTn_])
            if nj == 3:
                I("act", "activation", PTb[:, 2, :], slots[2], AF.Exp, scale=0.125, reads=[RB[sc_b]], writes=[PTn_])
            yield
            pv = PB[sc_a][:, 0:256]
            for idx, j in enumerate(js):
                I("pe", "matmul", pv, lhsT=svcx[j % 6][:, g, :], rhs=PTb[:, idx, :], start=(idx == 0), stop=False,
                  reads=["svcx%d" % (j % 6), PTn_], writes=[RB[sc_a]])
            I("pe", "matmul", pv, lhsT=zo[0:1, :], rhs=esink[0:1, g * 256:(g + 1) * 256], start=False, stop=True,
              reads=["zo", "esink"], writes=[RB[sc_a]])
            yield
            rd = rden[64:128, g * 256:(g + 1) * 256]
            I("act", "activation", rd, PB[sc_a][64:128, 0:256], AF.Ln, reads=[RB[sc_a]], writes=["rden%d" % g])
            I("act", "activation", rd, rd, AF.Exp, scale=-1.0, reads=["rden%d" % g], writes=["rden%d" % g])
            yield
            numv = PB[sc_a][0:64, 0:256].rearrange("p (r2 par q) -> p r2 par q", r2=2, par=2)
            rdv = rd.rearrange("p (r2 par q) -> p r2 par q", r2=2, par=2)
            for par in range(2):
                I("dve", "tensor_tensor", swaT2[p][par * 64:(par + 1) * 64, g * 2:(g + 1) * 2, cc * 64:(cc + 1) * 64],
                  numv[:, :, par, :], rdv[:, :, par, :], op=ALU.mult, reads=[RB[sc_a], "rden%d" % g], writes=["swaT%d" % p])
            yield

        def ret_gen(t):
            p = t % 2
            for h in range(4):
                I("pe", "matmul", PB[3][:, h * 128:(h + 1) * 128], lhsT=kT[p][:, h, :], rhs=qT[p][:, h, :], start=True, stop=True,
                  reads=["kT%d" % p, "qT%d" % p], writes=[RB[3]])
            yield
            I("dve", "tensor_tensor", scT[:, :, :], PB[3][:, :].rearrange("p (h n) -> p h n", h=4),
              C("cm").unsqueeze(1).broadcast_to([128, 4, 128]), op=ALU.mult, reads=[RB[3], "ctab"], writes=["scT"])
            yield
            for h in range(4):
                I("pe", "matmul", PB[4][:, h * 128:(h + 1) * 128], lhsT=scT[:, h, :], rhs=v_tm[p][:, h * 128:(h + 1) * 128],
                  start=True, stop=(t == 0), reads=["scT", "v_tm%d" % p], writes=[RB[4]])
                if t > 0:
                    I("pe", "matmul", PB[4][:, h * 128:(h + 1) * 128], lhsT=qT[p][:, h, :], rhs=Sg[:, h, :],
                      start=False, stop=True, reads=["qT%d" % p, "Sg"], writes=[RB[4]])
            for h in range(4):
                I("pe", "matmul", PB[3][:, h * 128:(h + 1) * 128], lhsT=k_tm[p][:, h * 128:(h + 1) * 128],
                  rhs=v_tm[p][:, h * 128:(h + 1) * 128], start=True, stop=True, reads=["k_tm%d" % p, "v_tm%d" % p], writes=[RB[3]])
            yield
            for h in range(4):
                I("dve", "scalar_tensor_tensor", S[:, h, :], S[:, h, :], GC128[h], PB[3][:, h * 128:(h + 1) * 128],
                  op0=ALU.mult, op1=ALU.add, reads=[RB[3], "S"], writes=["S"])
            if t < NTP - 1:
                for h in range(4):
                    I("act", "mul", Sg[:, h, :], S[:, h, :], GC128[h], reads=["S"], writes=["Sg"])
            else:
                finals.append(DMA("sp", ret_p.rearrange("h d e -> d h e"), S[:, :, :], reads=["S"]))
            for _ in groupnorm_and_mix(128, p):
                yield
            yield

        def swa_gen(t):
            p = t % 2
            bufs0 = (5, 6, 0, PT, "PT")
            bufs1 = (7, 6, 256, PT2, "PT2")
            for cc in range(2):
                ch = [swa_chain(t, p, 0, cc, bufs0), swa_chain(t, p, 1, cc, bufs1)]
                while ch:
                    for gch in list(ch):
                        try:
                            next(gch)
                        except StopIteration:
                            ch.remove(gch)
                    yield

        def back_prompt(t):
            gens = [ret_gen(t), swa_gen(t)]
            while gens:
                for gg in list(gens):
                    try:
                        next(gg)
                    except StopIteration:
                        gens.remove(gg)
                yield

        def back_sample():
            t = NTP
            p = t % 2
            for b in range(2):
                DMA("sp", S_s[b][:, :, :], st_in[b].rearrange("h d e -> d h e"), writes=["S" if b == 0 else "S_s1"])
                for h in range(4):
                    I("act", "mul", Sg_s[b][:, h, :], S_s[b][:, h, :], GC16[h], reads=["S" if b == 0 else "S_s1"],
                      writes=["Sg" if b == 0 else "Sg_s1"])
                I("pool", "memset", qTm[b][:, :, :], 0.0, writes=["qTm%d" % b])
                I("pool", "tensor_copy", qTm[b][:, :, b * 32:b * 32 + 16], qT[p][:, :, b * 32:b * 32 + 16],
                  reads=["qT%d" % p], writes=["qTm%d" % b])
            for h in range(4):
                I("pe", "matmul", PB[3][0:64, h * 128:h * 128 + 64], lhsT=kT[p][:, h, 0:64], rhs=qT[p][:, h, 0:64], start=True, stop=True,
                  reads=["kT%d" % p, "qT%d" % p], writes=[RB[3]])
            I("dve", "tensor_tensor", scT[0:64, :, 0:64], PB[3][0:64, :].rearrange("p (h n) -> p h n", h=4)[:, :, 0:64],
              C("cm_s", 64).unsqueeze(1).broadcast_to([64, 4, 64]), op=ALU.mult, reads=[RB[3], "ctab"], writes=["scT"])
            yield
            for h in range(4):
                I("pe", "matmul", PB[4][0:64, h * 128:(h + 1) * 128], lhsT=scT[0:64, h, 0:64], rhs=v_tm[p][0:64, h * 128:(h + 1) * 128],
                  start=True, stop=False, reads=["scT", "v_tm%d" % p], writes=[RB[4]])
                for b in range(2):
                    I("pe", "matmul", PB[4][0:64, h * 128:(h + 1) * 128], lhsT=qTm[b][:, h, :], rhs=Sg_s[b][:, h, :],
                      start=False, stop=(b == 1), reads=["qTm%d" % b, "Sg" if b == 0 else "Sg_s1"], writes=[RB[4]])
            for b in range(2):
                sres = "S" if b == 0 else "S_s1"
                for h in range(4):
                    I("pe", "matmul", PB[3][:, h * 128:(h + 1) * 128], lhsT=k_tm[p][b * 32:b * 32 + 16, h * 128:(h + 1) * 128],
                      rhs=v_tm[p][b * 32:b * 32 + 16, h * 128:(h + 1) * 128], start=True, stop=True,
                      reads=["k_tm%d" % p, "v_tm%d" % p], writes=[RB[3]])
                for h in range(4):
                    I("dve", "scalar_tensor_tensor", S_s[b][:, h, :], S_s[b][:, h, :], GC16[h], PB[3][:, h * 128:(h + 1) * 128],
                      op0=ALU.mult, op1=ALU.add, reads=[RB[3], sres], writes=[sres])
                finals.append(DMA("sp", ret_s[b].rearrange("h d e -> d h e"), S_s[b][:, :, :], reads=[sres]))
            yield
            for _ in groupnorm_and_mix(64, p):
                yield
            yield
            nb = C("nb_s", 64)
            for b in range(2):
                DMA("pool", ckb[:, :], ck[b], writes=["ckb"])
                DMA("pool", cvx[b][:, :, 0:64], cv[b].rearrange("k (g d) -> k g d", g=2), writes=["cvx%d" % b])
                for g in range(2):
                    I("pe", "transpose", PBb[5][0:64, g * 128:(g + 1) * 128], ckb[:, g * 64:(g + 1) * 64], ident[:, :],
                      reads=["ckb", "ident"], writes=[RB[5]])
                I("act", "copy", ckT[b], PBb[5][0:64, 0:256].rearrange("p (g n) -> p g n", g=2),
                  reads=[RB[5]], writes=["PT"])
            skT_t = skT[t % 3]
            for b in range(2):
                for g in range(2):
                    rhs_q = sqT[p][:, g * 4:(g + 1) * 4, b * 32:b * 32 + 16]
                    I("pe", "matmul", PB[5][:, 0:64], lhsT=ckT[b][:, g, :], rhs=rhs_q, start=True, stop=True,
                      reads=["PT", "sqT%d" % p], writes=[RB[5]])
                    I("pe", "matmul", PB[6][0:64, 0:64], lhsT=skT_t[:, g, 0:64], rhs=rhs_q, start=True, stop=True,
                      reads=["skT%d" % (t % 3), "sqT%d" % p], writes=[RB[6]])
                    I("act", "activation", PTc, PB[5][:, 0:64], AF.Exp, scale=0.125, reads=[RB[5]], writes=["scT"])
                    I("act", "activation", PTn, PB[6][0:64, 0:64], AF.Exp, bias=nb[:, b:b + 1], scale=0.125,
                      reads=[RB[6], "ctab"], writes=["PT"])
                    pv = PB[7][:, 0:64]
                    I("pe", "matmul", pv, lhsT=cvx[b][:, g, :], rhs=PTc, start=True, stop=False,
                      reads=["cvx%d" % b, "scT"], writes=[RB[7]])
                    I("pe", "matmul", pv, lhsT=svx_s[:, g, :], rhs=PTn, start=False, stop=False,
                      reads=["svcx%d" % SVS, "PT"], writes=[RB[7]])
                    I("pe", "matmul", pv, lhsT=zo[0:1, :], rhs=esink16[0:1, g * 64:(g + 1) * 64], start=False, stop=True,
                      reads=["zo", "esink16"], writes=[RB[7]])
                    I("act", "activation", rden[64:128, 0:64], PB[7][64:128, 0:64], AF.Ln, reads=[RB[7]], writes=["rden0"])
                    I("act", "activation", rden[64:128, 0:64], rden[64:128, 0:64], AF.Exp, scale=-1.0, reads=["rden0"], writes=["rden0"])
                    numv = PB[7][0:64, 0:64].rearrange("p (r2 par q) -> p r2 par q", r2=2, par=2)
                    rdv = rden[64:128, 0:64].rearrange("p (r2 par q) -> p r2 par q", r2=2, par=2)
                    for par in range(2):
                        I("dve", "tensor_tensor", swaT2[p][par * 64:(par + 1) * 64, g * 2:(g + 1) * 2, b * 32:b * 32 + 16],
                          numv[:, :, par, :], rdv[:, :, par, :], op=ALU.mult, reads=[RB[7], "rden0"], writes=["swaT%d" % p])
                yield

        def run_interleaved(gens, ratio=1):
            gens = [g for g in gens if g is not None]
            k = 0
            while gens:
                for gi, g in enumerate(list(gens)):
                    if gi == 0 and len(gens) > 1 and (k % ratio) != 0:
                        continue
                    try:
                        next(g)
                    except StopIteration:
                        gens.remove(g)
                k += 1

        def chain_gens(*gs):
            for g in gs:
                if g is not None:
                    for _ in g:
                        yield

        run_interleaved([front(0)])
        for t in range(NT):
            if t + 1 < NT:
                tl = None
                if t >= 1:
                    tl = ln1_only(t - 1) if (t - 1) <= NTP - 2 else out_proj_ln1(t - 1)
                a = chain_gens(front(t + 1), tl)
            else:
                a = chain_gens(out_proj_ln1(t - 1))
            bk = back_prompt(t) if t < NTP else back_sample()
            run_interleaved([a, bk])
        run_interleaved([out_proj_ln1(NT - 1)])

        def wup(bi, kc):
            if bi == 0:
                return WA[:, kc * 1024:(kc + 1) * 1024]
            if kc < 6:
                return WA[:, 16384 + kc * 1024:16384 + (kc + 1) * 1024]
            return yacc_spill[kc - 6]

        def wdn(bi, fc):
            if bi == 0:
                return WA[:, 8192 + fc * 1024:8192 + (fc + 1) * 1024]
            return WB[:, fc * 1024:(fc + 1) * 1024]

        def load_mlp_weights(grp, extra_w=()):
            bi = grp % 2
            gate = []
            if extra_w:
                I("pool", "memset", gsc[:, :], 0.0, reads=[], writes=list(extra_w) + ["wgate%d" % grp])
                gate = ["wgate%d" % grp]
            for kc in range(8):
                DMA("pool", wup(bi, kc), w_up[kc * 128:(kc + 1) * 128, grp * 1024:(grp + 1) * 1024],
                    reads=gate, writes=["wup%d_%d" % (bi, kc)])
            for fc in range(8):
                DMA("pool", wdn(bi, fc), w_down[grp * 1024 + fc * 128:grp * 1024 + (fc + 1) * 128, :],
                    reads=gate, writes=["wdn%d_%d" % (bi, fc)])

        load_mlp_weights(0, extra_w=WIN_ALL)
        P.barrier(skip=("wup", "wdn", "wgate"))
        es1.close()

        NCOL = NTP * 128 + 64
        x1T = sb("x1T", [128, 8, NCOL], BF16)
        hT = sb("hT", [128, 8, 512], BF16)
        hr = [sb("hr%d" % i, [128, 512], BF16) for i in range(2)]
        x1bf = [sb("x1bf%d" % i, [128, D], BF16) for i in range(2)]
        wsp = sb("wsp", [128, 2048], BF16)
        yacc_spill = [wsp[:, 0:1024], wsp[:, 1024:2048]]
        print("sbuf remaining after phase-2 alloc:", nc.sbuf_bytes_remaining)

        DMA("sp", lnw_b[:, :], ln2w.broadcast_to([128, D]), writes=["lnw_b"])
        DMA("sp", lnb_b[:, :], ln2b.broadcast_to([128, D]), writes=["lnb_b"])

        blocks = [(NTP * 128, 64, [NTP])]
        for i in range(0, NTP, 4):
            tl = list(range(i, min(i + 4, NTP)))
            blocks.append((i * 128, 128 * len(tl), tl))
        nbuild = [0]

        def build_x1T(tiles):
            for t in tiles:
                Rr = 64 if t == NTP else 128
                k = nbuild[0] % 2
                nbuild[0] += 1
                bank = 0 if k == 0 else 7
                I("act", "copy", x1bf[k][0:Rr, :], yacc[0:Rr, t, :], reads=["yacc%d" % t], writes=["x1bf%d" % k])
                for kc in range(8):
                    I("pe", "transpose", PBb[bank][:, kc * 128:kc * 128 + Rr], x1bf[k][0:Rr, kc * 128:(kc + 1) * 128],
                      ident[0:Rr, 0:Rr], reads=["x1bf%d" % k, "ident"], writes=[RB[bank]])
                I("dve", "tensor_copy", x1T[:, :, t * 128:t * 128 + Rr],
                  PBb[bank][:, :].rearrange("p (k n) -> p k n", k=8)[:, :, 0:Rr], reads=[RB[bank]], writes=["x1T%d" % t])

        def epilogue(t):
            Rr = 64 if t == NTP else 128
            for _ in layer_norm(yacc[:, t, :], ["yacc%d" % t], Rr, aff="dve"):
                pass
            if t < NTP:
                finals.append(DMA("sp", y_p[t * 128:(t + 1) * 128, :], yacc[:, t, :], reads=["yacc%d" % t]))
            else:
                for b in range(2):
                    finals.append(DMA("sp", y_s[b], yacc[b * 32:b * 32 + 16, t, :], reads=["yacc%d" % t]))

        nhr = [0]
        pending = []

        def mlp_pass(grp, bidx, final_block):
            bi = grp % 2
            c0, ncol, tiles = blocks[bidx]
            if grp == 0 and bidx == 0:
                build_x1T(tiles)
            for fc in range(8):
                bank = 1 + (fc % 2)
                for kc in range(8):
                    I("pe", "matmul", PB[bank][:, 0:ncol], lhsT=wup(bi, kc)[:, fc * 128:(fc + 1) * 128], rhs=x1T[:, kc, c0:c0 + ncol],
                      start=(kc == 0), stop=(kc == 7), reads=["wup%d_%d" % (bi, kc)] + ["x1T%d" % tt for tt in tiles],
                      writes=[RB[bank]])
                hrb = hr[nhr[0] % 2]
                hrn = "hr%d" % (nhr[0] % 2)
                nhr[0] += 1
                I("act", "activation", hrb[:, 0:ncol], PB[bank][:, 0:ncol], AF.Relu, reads=[RB[bank]], writes=[hrn])
                I("pool", "tensor_tensor", hT[:, fc, 0:ncol], hrb[:, 0:ncol], hrb[:, 0:ncol], op=ALU.mult,
                  reads=[hrn], writes=["hT%d" % fc])
                if pending and fc % 2 == 1:
                    epilogue(pending.pop(0))
            if grp == 0 and bidx + 1 < len(blocks):
                build_x1T(blocks[bidx + 1][2])
            if grp == 0 and bidx == 0:
                load_mlp_weights(1, extra_w=WO_ALL)
            for ti, t in enumerate(tiles):
                Rr = 64 if t == NTP else 128
                for hf in range(2):
                    bank = 3 + 2 * (ti % 2) + hf
                    for fc in range(8):
                        I("pe", "matmul", PB[bank][0:Rr, :], lhsT=hT[:, fc, ti * 128:ti * 128 + Rr],
                          rhs=wdn(bi, fc)[:, hf * 512:(hf + 1) * 512], start=(fc == 0), stop=(fc == 7),
                          reads=["hT%d" % fc, "wdn%d_%d" % (bi, fc)], writes=[RB[bank]])
                    ysl = yacc[0:Rr, t, hf * 512:(hf + 1) * 512]
                    if grp == 0:
                        I("dve", "scalar_tensor_tensor", ysl, ysl, ALPHA, PB[bank][0:Rr, :], op0=ALU.mult, op1=ALU.add,
                          reads=[RB[bank], "yacc%d" % t], writes=["yacc%d" % t])
                    else:
                        I("dve", "tensor_tensor", ysl, ysl, PB[bank][0:Rr, :], op=ALU.add,
                          reads=[RB[bank], "yacc%d" % t], writes=["yacc%d" % t])
                if grp == 3 and final_block:
                    while pending:
                        epilogue(pending.pop(0))
                    pending.append(t)
            if grp == 3 and not final_block:
                pending.extend(tiles)

        nb = len(blocks)
        for bidx in range(nb):
            mlp_pass(0, bidx, False)
        load_mlp_weights(2)
        for bidx in range(nb):
            mlp_pass(1, bidx, False)
        load_mlp_weights(3)
        lead = min(3, nb)
        for bidx in range(lead):
            mlp_pass(2, bidx, False)
        for bidx in range(lead):
            mlp_pass(3, bidx, bidx == nb - 1)
        for bidx in range(lead, nb):
            mlp_pass(2, bidx, False)
            mlp_pass(3, bidx, bidx == nb - 1)
        while pending:
            epilogue(pending.pop(0))

        P.emit(nc, finals)
        print("ops per engine:", P.stats)
    return nc


_NC_CACHE = {}


def kernel(x_prompt, x_sample, cache_swa_k, cache_swa_v, state_ret, w_in, ret_gn_w, swa_sinks,
           w_out, ln1_w, ln1_b, w_up, w_down, ln2_w, ln2_b):
    f = lambda a: np.ascontiguousarray(np.asarray(a, dtype=np.float32))
    x_prompt, x_sample = f(x_prompt), f(x_sample)
    cache_swa_k, cache_swa_v, state_ret = f(cache_swa_k), f(cache_swa_v), f(state_ret)
    if "nc" not in _NC_CACHE:
        _NC_CACHE["nc"] = build_nc()
    nc = _NC_CACHE["nc"]
    ctab, rtab = make_tables()
    ident = np.eye(128, dtype=np.float32)
    shared = {
        "w_in": f(w_in[0]), "w_out": f(w_out[0]), "w_up": f(w_up[0]), "w_down": f(w_down[0]),
        "gnw": f(ret_gn_w[0]).reshape(1, 512), "sinks": f(swa_sinks[0]).reshape(1, 8),
        "ln1w": f(ln1_w[0]).reshape(1, D), "ln1b": f(ln1_b[0]).reshape(1, D),
        "ln2w": f(ln2_w[0]).reshape(1, D), "ln2b": f(ln2_b[0]).reshape(1, D),
        "ctab": ctab, "rtab": rtab, "ident": ident,
    }
    in_maps = []
    for c in range(8):
        m = dict(shared)
        m["xp"] = x_prompt[c]
        m["xs"] = x_sample[2 * c:2 * c + 2]
        m["ck"] = cache_swa_k[0, 2 * c:2 * c + 2].reshape(2, 128, 128)
        m["cv"] = cache_swa_v[0, 2 * c:2 * c + 2].reshape(2, 128, 128)
        m["st"] = state_ret[0, 2 * c:2 * c + 2]
        in_maps.append(m)
    res = run_bass_kernel_spmd(nc, in_maps, core_ids=list(range(8)))
    rs = res.results
    y_p = np.stack([rs[c]["y_p"] for c in range(8)], axis=0)
    y_s = np.concatenate([rs[c]["y_s"] for c in range(8)], axis=0)
    k_p = np.stack([rs[c]["k_p"].reshape(128, 2, 64) for c in range(8)], axis=0)[None]
    v_p = np.stack([rs[c]["v_p"].reshape(128, 2, 64) for c in range(8)], axis=0)[None]
    r_p = np.stack([rs[c]["ret_p"] for c in range(8)], axis=0)[None]
    k_s = np.concatenate([rs[c]["k_s"].reshape(2, 16, 2, 64) for c in range(8)], axis=0)[None]
    v_s = np.concatenate([rs[c]["v_s"].reshape(2, 16, 2, 64) for c in range(8)], axis=0)[None]
    r_s = np.concatenate([rs[c]["ret_s"] for c in range(8)], axis=0)[None]
    return tuple(np.ascontiguousarray(a, dtype=np.float32) for a in (y_p, y_s, k_p, v_p, r_p, k_s, v_s, r_s))
```
